# Optimizing a Trainium2 kernel written in Bass

```python
import math
import jax, jax.numpy as jnp
from jax import lax
import numpy as np

D_MODEL = 2048
BATCH = 4
SEQ = 4096
DEPTH = 1
DEC_BATCH = 4
DEC_SEQ = 2048
PAST_LEN = 128

HEAD_DIM = 128
N_Q_HEADS = 8
N_KV_HEADS = 2
Q_PER_KV = N_Q_HEADS // N_KV_HEADS
ATTN_WIDTH = N_Q_HEADS * HEAD_DIM
KV_WIDTH = N_KV_HEADS * HEAD_DIM
WINDOW = 128
BLOCK = 128
ROPE_THETA = 10000.0
SSM_WIDTH = D_MODEL // 2
SSM_GROUP = 16
SSM_GROUPS = SSM_WIDTH // SSM_GROUP
SSM_STATE = 64
DT_MIN = 1e-3
DT_MAX = 1e-1
D_FF = 5632
IN_WIDTH = ATTN_WIDTH + 2 * KV_WIDTH + SSM_WIDTH + 2 * D_MODEL
SPLITS = (ATTN_WIDTH, ATTN_WIDTH + KV_WIDTH, ATTN_WIDTH + 2 * KV_WIDTH,
          ATTN_WIDTH + 2 * KV_WIDTH + SSM_WIDTH, ATTN_WIDTH + 2 * KV_WIDTH + SSM_WIDTH + D_MODEL)
EPS = 1e-6

kernel_name = 'hybrid_bidir_swa_s5_macaron_encoder'


def rmsnorm(x, gain):
    xf = x.astype(jnp.float32)
    y = xf * lax.rsqrt(jnp.mean(xf * xf, axis=-1, keepdims=True) + EPS)
    return (y * gain.astype(jnp.float32)).astype(x.dtype)


def swiglu(x, w_gate_up, w_down):
    gate, up = jnp.split(x @ w_gate_up, 2, axis=-1)
    return (jax.nn.silu(gate) * up) @ w_down


def rope(x, pos):
    d = x.shape[-1]
    inv_freq = ROPE_THETA ** (-jnp.arange(0, d, 2, dtype=jnp.float32) / d)
    ang = pos.astype(jnp.float32)[:, None] * inv_freq[None, :]
    cos = jnp.cos(ang)[:, None, :]
    sin = jnp.sin(ang)[:, None, :]
    xf = x.astype(jnp.float32)
    x1, x2 = jnp.split(xf, 2, axis=-1)
    out = jnp.concatenate([x1 * cos - x2 * sin, x1 * sin + x2 * cos], axis=-1)
    return out.astype(x.dtype)


def windowed_gqa(q, k, v, sink):
    b, s, _, d = q.shape
    nb = s // BLOCK
    qb = q.reshape(b, nb, BLOCK, N_KV_HEADS, Q_PER_KV, d)
    pad = ((0, 0), (BLOCK, BLOCK), (0, 0), (0, 0))

    def band(t):
        tb = jnp.pad(t, pad).reshape(b, nb + 2, BLOCK, N_KV_HEADS, d)
        return jnp.concatenate([tb[:, :-2], tb[:, 1:-1], tb[:, 2:]], axis=2)

    kw = band(k)
    vw = band(v)
    scores = jnp.einsum('bnqhgd,bnkhd->bnhgqk', qb, kw,
                        preferred_element_type=jnp.float32) * (d ** -0.5)
    qi = jnp.arange(BLOCK)[:, None]
    kj = jnp.arange(3 * BLOCK)[None, :]
    rel_ok = jnp.abs(kj - BLOCK - qi) <= WINDOW
    kpos = jnp.arange(nb)[:, None] * BLOCK - BLOCK + jnp.arange(3 * BLOCK)[None, :]
    in_range = (kpos >= 0) & (kpos < s)
    mask = rel_ok[None, :, :] & in_range[:, None, :]
    scores = jnp.where(mask[None, :, None, None], scores, -jnp.inf)
    sink_l = sink.astype(jnp.float32).reshape(N_KV_HEADS, Q_PER_KV)[None, None, :, :, None]
    m = jnp.maximum(jnp.max(scores, axis=-1), sink_l)
    p = jnp.exp(scores - m[..., None])
    denom = jnp.sum(p, axis=-1) + jnp.exp(sink_l - m)
    probs = (p / denom[..., None]).astype(v.dtype)
    out = jnp.einsum('bnhgqk,bnkhd->bnqhgd', probs, vw)
    return out.reshape(b, s, N_Q_HEADS * HEAD_DIM)


def _ssm_combine(e1, e2):
    a1, b1 = e1
    a2, b2 = e2
    return a1 * a2, a2 * b1 + b2


def s5_bidirectional(u, lam_re, lam_im, log_dt, b_re, b_im, c_re, c_im, d_skip):
    bsz, s, _ = u.shape
    uf = u.astype(jnp.float32).reshape(bsz, s, SSM_GROUPS, SSM_GROUP)
    uc = uf.astype(jnp.complex64)
    y = d_skip.astype(jnp.float32).reshape(SSM_GROUPS, SSM_GROUP) * uf
    for direction in range(2):
        lam = lax.complex(lam_re[direction].astype(jnp.float32), lam_im[direction].astype(jnp.float32))
        dt = jnp.exp(log_dt[direction].astype(jnp.float32))[:, None]
        lam_bar = jnp.exp(lam * dt)
        bmat = lax.complex(b_re[direction].astype(jnp.float32), b_im[direction].astype(jnp.float32))
        b_bar = ((lam_bar - 1.0) / lam)[..., None] * bmat
        bu = jnp.einsum('bsgc,gpc->bsgp', uc, b_bar)
        a = jnp.broadcast_to(lam_bar, bu.shape)
        _, states = lax.associative_scan(_ssm_combine, (a, bu), reverse=(direction == 1), axis=1)
        cmat = lax.complex(c_re[direction].astype(jnp.float32), c_im[direction].astype(jnp.float32))
        y = y + jnp.real(jnp.einsum('bsgp,gcp->bsgc', states, cmat))
    return y.reshape(bsz, s, SSM_WIDTH).astype(u.dtype)


def encoder_layer(x, ffn1_norm, ffn1_w_gate_up, ffn1_w_down, mix_norm, w_in, q_norm, k_norm, attn_sink,
                  ssm_lambda_re, ssm_lambda_im, ssm_log_dt, ssm_b_re, ssm_b_im, ssm_c_re, ssm_c_im, ssm_d,
                  w_glu, b_glu, w_attn_out, w_ssm_out, w_out, ffn2_norm, ffn2_w_gate_up, ffn2_w_down):
    b, s, _ = x.shape
    x = x + 0.5 * swiglu(rmsnorm(x, ffn1_norm), ffn1_w_gate_up, ffn1_w_down)
    h = rmsnorm(x, mix_norm)
    q, k, v, u, g_attn, g_ssm = jnp.split(h @ w_in, SPLITS, axis=-1)
    pos = jnp.arange(s)
    q = rope(rmsnorm(q.reshape(b, s, N_Q_HEADS, HEAD_DIM), q_norm), pos)
    k = rope(rmsnorm(k.reshape(b, s, N_KV_HEADS, HEAD_DIM), k_norm), pos)
    v = v.reshape(b, s, N_KV_HEADS, HEAD_DIM)
    attn_branch = windowed_gqa(q, k, v, attn_sink) @ w_attn_out
    z = jax.nn.gelu(s5_bidirectional(u, ssm_lambda_re, ssm_lambda_im, ssm_log_dt,
                                     ssm_b_re, ssm_b_im, ssm_c_re, ssm_c_im, ssm_d))
    z = z * jax.nn.sigmoid(z @ w_glu + b_glu)
    ssm_branch = z @ w_ssm_out
    merged = jax.nn.sigmoid(g_attn) * attn_branch + jax.nn.sigmoid(g_ssm) * ssm_branch
    x = x + merged @ w_out
    x = x + 0.5 * swiglu(rmsnorm(x, ffn2_norm), ffn2_w_gate_up, ffn2_w_down)
    return x


def setup_inputs(seed: int = 0) -> dict:
    key = jax.random.key(seed)
    ks = jax.random.split(key, 32)
    f32 = jnp.float32
    L = DEPTH

    def nrm(k, shape, scale):
        return jax.random.normal(k, shape, f32) * scale

    def gain(k, shape):
        return 1.0 + 0.02 * jax.random.normal(k, shape, f32)

    n_idx = jnp.arange(SSM_STATE, dtype=f32)
    lam_re = -0.5 + 0.01 * jax.random.normal(ks[10], (L, 2, SSM_GROUPS, SSM_STATE), f32)
    lam_im = math.pi * n_idx + 0.01 * jax.random.normal(ks[11], (L, 2, SSM_GROUPS, SSM_STATE), f32)
    log_dt = jax.random.uniform(ks[12], (L, 2, SSM_GROUPS), f32, math.log(DT_MIN), math.log(DT_MAX))
    b_scale = (2.0 * SSM_GROUP) ** -0.5
    c_scale = (2.0 * SSM_STATE) ** -0.5
    return {
        'x_prompt': jax.random.normal(ks[0], (BATCH, SEQ, D_MODEL), f32),
        'x_sample': jax.random.normal(ks[1], (DEC_BATCH, DEC_SEQ, D_MODEL), f32),
        'ffn1_norm': gain(ks[2], (L, D_MODEL)),
        'ffn1_w_gate_up': nrm(ks[3], (L, D_MODEL, 2 * D_FF), D_MODEL ** -0.5),
        'ffn1_w_down': nrm(ks[4], (L, D_FF, D_MODEL), D_FF ** -0.5),
        'mix_norm': gain(ks[5], (L, D_MODEL)),
        'w_in': nrm(ks[6], (L, D_MODEL, IN_WIDTH), D_MODEL ** -0.5),
        'q_norm': gain(ks[7], (L, HEAD_DIM)),
        'k_norm': gain(ks[8], (L, HEAD_DIM)),
        'attn_sink': nrm(ks[9], (L, N_Q_HEADS), 0.5),
        'ssm_lambda_re': lam_re,
        'ssm_lambda_im': lam_im,
        'ssm_log_dt': log_dt,
        'ssm_b_re': nrm(ks[13], (L, 2, SSM_GROUPS, SSM_STATE, SSM_GROUP), b_scale),
        'ssm_b_im': nrm(ks[14], (L, 2, SSM_GROUPS, SSM_STATE, SSM_GROUP), b_scale),
        'ssm_c_re': nrm(ks[15], (L, 2, SSM_GROUPS, SSM_GROUP, SSM_STATE), c_scale),
        'ssm_c_im': nrm(ks[16], (L, 2, SSM_GROUPS, SSM_GROUP, SSM_STATE), c_scale),
        'ssm_d': nrm(ks[17], (L, SSM_WIDTH), 1.0),
        'w_glu': nrm(ks[18], (L, SSM_WIDTH, SSM_WIDTH), SSM_WIDTH ** -0.5),
        'b_glu': nrm(ks[19], (L, SSM_WIDTH), 0.01),
        'w_attn_out': nrm(ks[20], (L, ATTN_WIDTH, D_MODEL), ATTN_WIDTH ** -0.5),
        'w_ssm_out': nrm(ks[21], (L, SSM_WIDTH, D_MODEL), SSM_WIDTH ** -0.5),
        'w_out': nrm(ks[22], (L, D_MODEL, D_MODEL), D_MODEL ** -0.5),
        'ffn2_norm': gain(ks[23], (L, D_MODEL)),
        'ffn2_w_gate_up': nrm(ks[24], (L, D_MODEL, 2 * D_FF), D_MODEL ** -0.5),
        'ffn2_w_down': nrm(ks[25], (L, D_FF, D_MODEL), D_FF ** -0.5),
    }


def reference(x_prompt, x_sample, ffn1_norm, ffn1_w_gate_up, ffn1_w_down, mix_norm, w_in, q_norm, k_norm,
              attn_sink, ssm_lambda_re, ssm_lambda_im, ssm_log_dt, ssm_b_re, ssm_b_im, ssm_c_re, ssm_c_im,
              ssm_d, w_glu, b_glu, w_attn_out, w_ssm_out, w_out, ffn2_norm, ffn2_w_gate_up, ffn2_w_down):
    y_prompt = x_prompt
    y_sample = x_sample
    for l in range(DEPTH):
        params = (ffn1_norm[l], ffn1_w_gate_up[l], ffn1_w_down[l], mix_norm[l], w_in[l], q_norm[l], k_norm[l],
                  attn_sink[l], ssm_lambda_re[l], ssm_lambda_im[l], ssm_log_dt[l], ssm_b_re[l], ssm_b_im[l],
                  ssm_c_re[l], ssm_c_im[l], ssm_d[l], w_glu[l], b_glu[l], w_attn_out[l], w_ssm_out[l], w_out[l],
                  ffn2_norm[l], ffn2_w_gate_up[l], ffn2_w_down[l])
        y_prompt = encoder_layer(y_prompt, *params)
        y_sample = encoder_layer(y_sample, *params)
    return (y_prompt, y_sample)
```

```python
import numpy as np
import concourse.bass as bass
import concourse.mybir as mybir
from concourse.bass_utils import run_bass_kernel_spmd

F32 = mybir.dt.float32
BF16 = mybir.dt.bfloat16
AF = mybir.ActivationFunctionType
ALU = mybir.AluOpType
AX = mybir.AxisListType

D = 2048
DFF = 5632
NFF = DFF // 128
INW = 6656
EPS = 1e-6


class _Op:
    __slots__ = ("eng", "fn", "waits", "signal", "count", "dma")

    def __init__(self, eng, fn, waits, dma):
        self.eng = eng
        self.fn = fn
        self.waits = waits
        self.signal = False
        self.count = 0
        self.dma = dma


class Sched:
    ENGS = ("pe", "act", "dve", "pool", "sp")

    def __init__(self, nc):
        self.nc = nc
        self.ops = {e: [] for e in self.ENGS}
        self.state = {}
        self.dma_count = {}

    @staticmethod
    def _src(tok):
        return ("e", tok[1]) if tok[0] == "eng" else ("d", tok[1])

    @staticmethod
    def _newer(a, b):
        return a[2] > b[2]

    def op(self, eng, fn, reads=(), writes=(), pwrites=(), dma=None):
        deps = {}

        def add(tok):
            s = self._src(tok)
            if s not in deps or self._newer(tok, deps[s]):
                deps[s] = tok

        for b in reads:
            st = self.state.get(b)
            if st:
                for t in st[0].values():
                    add(t)
        for b in list(writes) + list(pwrites):
            st = self.state.get(b)
            if st:
                for t in st[0].values():
                    add(t)
                for t in st[1].values():
                    add(t)
        o = _Op(eng, fn, list(deps.values()), dma)
        idx = len(self.ops[eng])
        self.ops[eng].append(o)
        if dma is not None:
            n = self.dma_count.get(dma, 0) + 1
            self.dma_count[dma] = n
            tok = ("dma", dma, 16 * n)
        else:
            tok = ("eng", eng, idx)
        s = self._src(tok)
        for b in reads:
            st = self.state.setdefault(b, [{}, {}])
            st[1][s] = tok
        for b in writes:
            self.state[b] = [{s: tok}, {}]
        for b in pwrites:
            st = self.state.setdefault(b, [{}, {}])
            st[0][s] = tok
        return tok

    def barrier(self):
        toks = []
        for e in self.ENGS:
            for i in range(len(self.ops[e]) - 1, -1, -1):
                if self.ops[e][i].dma is None:
                    toks.append(("eng", e, i))
                    break
        for k, n in self.dma_count.items():
            toks.append(("dma", k, 16 * n))
        for e in self.ENGS:
            self.ops[e].append(_Op(e, lambda eng: eng.nop(), list(toks), None))
        self.state = {}

    def emit(self, final_wait_keys=()):
        nc = self.nc
        for e in self.ENGS:
            for o in self.ops[e]:
                for t in o.waits:
                    if t[0] == "eng":
                        self.ops[t[1]][t[2]].signal = True
        for e in self.ENGS:
            c = 0
            for o in self.ops[e]:
                if o.signal:
                    c += 1
                o.count = c
        engobj = {"pe": nc.tensor, "act": nc.scalar, "dve": nc.vector, "pool": nc.gpsimd, "sp": nc.sync}
        import contextlib
        with contextlib.ExitStack() as es:
            esem = {e: es.enter_context(nc.semaphore("s_" + e)) for e in self.ENGS}
            dsem = {k: es.enter_context(nc.semaphore("d_%d" % i)) for i, k in enumerate(self.dma_count)}
            block = es.enter_context(nc.Block())
            ops = self.ops
            dma_count = self.dma_count

            def run(e, eng):
                known = {}
                for o in ops[e]:
                    need = {}
                    for t in o.waits:
                        if t[0] == "eng":
                            if t[1] == e and e == "pe":
                                continue
                            src = ops[t[1]][t[2]]
                            key = ("e", t[1])
                            val = src.count
                        else:
                            key = ("d", t[1])
                            val = t[2]
                        if val > need.get(key, 0):
                            need[key] = val
                    for key, val in need.items():
                        if known.get(key, 0) >= val:
                            continue
                        known[key] = val
                        sem = esem[key[1]] if key[0] == "e" else dsem[key[1]]
                        eng.wait_ge(sem, val)
                    ins = o.fn(eng)
                    if o.dma is not None:
                        ins.then_inc(dsem[o.dma], 16)
                    elif o.signal:
                        ins.then_inc(esem[e], 1)
                if e == "sp":
                    for k in dma_count:
                        eng.wait_ge(dsem[k], 16 * dma_count[k])

            @block.tensor
            def _(eng):
                run("pe", eng)

            @block.scalar
            def _(eng):
                run("act", eng)

            @block.vector
            def _(eng):
                run("dve", eng)

            @block.gpsimd
            def _(eng):
                run("pool", eng)

            @block.sync
            def _(eng):
                run("sp", eng)


MAGIC = 12582912.0
TWO_PI = float(2.0 * np.pi)
WSLOT = 12 * 1024
NWS = 4


def _cap(base, part0, nparts, col_off, dims):
    row = base.ap[0][0]
    return bass.AP(tensor=base.tensor, offset=part0 * row + col_off, ap=[[row, nparts]] + [list(d) for d in dims])


class Region:
    def __init__(self, prog, base, tag):
        self.p, self.off, self.tag, self.base = prog, base, tag, base

    def alloc(self, name, shape, dtype):
        nbytes = int(np.prod(shape[1:])) * mybir.dt.size(dtype)
        off = (self.off + 31) // 32 * 32
        self.off = off + nbytes
        assert self.off <= 229376, (self.tag, name, self.off)
        self.p.uid += 1
        h = self.p.nc.alloc_sbuf_tensor_at("%s_%s_%d" % (self.tag, name, self.p.uid), list(shape), dtype, offset=off)
        return h.ap()


class Prog:
    def __init__(self, S, dbg=(), phases="PASB", extin=()):
        self.extin = set(extin)
        import os
        self.dbg_barrier = bool(int(os.environ.get("DBG_BARRIER", "0")))
        self.S = S
        self.NT = S // 512
        self.NB = S // 128
        self.dbg = set(dbg)
        self.phases = phases
        self.nc = bass.Bass("TRN2", target_bir_lowering=False)
        self.sc = Sched(self.nc)
        self.uid = 0
        self.outkeys = []
        self.bankrr = 0
        self.build()

    def din(self, name, shape, dtype=F32):
        return self.nc.dram_tensor(name, list(shape), dtype, kind="ExternalInput").ap()

    def dscr(self, name, shape, dtype):
        kind = "ExternalOutput" if name in self.dbg else "Internal"
        if name in self.extin:
            kind = "ExternalInput"
        if name in self.dbg:
            self.outkeys.append(("w" + name) if name.startswith("s_") else name)
        return self.nc.dram_tensor(name, list(shape), dtype, kind=kind).ap()

    def op(self, eng, fn, reads=(), writes=(), pwrites=()):
        return self.sc.op(eng, fn, reads=reads, writes=writes, pwrites=pwrites)

    def dma(self, eng, out, in_, key, reads=(), writes=(), pwrites=(), nc_ok=False):
        if nc_ok:
            fn = lambda e: e.dma_start(out=out, in_=in_, allow_slow_non_contiguous=True)
        else:
            fn = lambda e: e.dma_start(out=out, in_=in_)
        return self.sc.op(eng, fn, reads=reads, writes=writes, pwrites=pwrites, dma=key)

    def bank(self):
        b = self.bankrr
        self.bankrr = (b + 1) % 8
        return b

    def wslot_view(self, slot, shape, dtype=BF16):
        n = int(np.prod(shape[1:]))
        assert n * mybir.dt.size(dtype) <= WSLOT
        base = self.wring[slot]
        flat = base[:, 0:n] if dtype == BF16 else base.bitcast(F32)[:, 0:n]
        names = "abcdefg"[: len(shape) - 1]
        if len(shape) == 2:
            return flat
        pat = "p (" + " ".join(names) + ") -> p " + " ".join(names)
        kw = {names[i]: shape[i + 1] for i in range(len(names))}
        return flat.rearrange(pat, **kw)

    def wstream(self, descs):
        st = {"issued": 0}
        prog = self

        def issue(i):
            shape, parts = descs[i]
            slot = prog.wcount % NWS
            prog.wcount += 1
            v = prog.wslot_view(slot, shape)
            for sel, src, skey in parts:
                prog.dma("sp", sel(v), src, "wr%d" % slot, reads=[skey], pwrites=["wr%d" % slot])
            return v, "wr%d" % slot

        got = {}

        def get(i):
            while st["issued"] < min(len(descs), i + NWS):
                got[st["issued"]] = issue(st["issued"])
                st["issued"] += 1
            return got.pop(i)

        return get

    def build(self):
        S, NT = self.S, self.NT
        self.x = self.din("x", [S, D])
        self.wdefs = {
            "f1gu": (D, 2 * DFF), "f1dn": (DFF, D), "win": (D, INW), "wglu": (1024, 1024),
            "wao": (1024, D), "wso": (1024, D), "wo": (D, D), "f2gu": (D, 2 * DFF), "f2dn": (DFF, D)}
        self.wsrc = {n: self.din("w_" + n, [k, c]) for n, (k, c) in self.wdefs.items()}
        self.gT3_d = self.din("gT3", [128, 3, 16])
        self.small_d = self.din("small", [128, 32])
        self.ident_d = self.din("ident", [128, 128], BF16)
        self.ones_d = self.din("onesm", [128, 128], BF16)
        self.rotm_d = self.din("rotm", [128, 128])
        self.cos_d = self.din("cosT", [128, S])
        self.sin_d = self.din("sinT", [128, S])
        self.tri_d = self.din("trimask", [128, 2, 128], BF16)
        self.valid_d = self.din("validcol", [128, self.NB])
        self.vrep_d = self.din("validrep", [S, 128], BF16)
        self.lp_pl_d = self.din("lp_pl", [128, 3, 1024])
        self.b_pl_d = self.din("b_pl", [128, 2, 1024])
        self.lp_sl_d = self.din("lp_sl", [128, 3, 64])
        self.c_sl_d = self.din("c_sl", [128, 2, 64, 16])
        self.b_sl_d = self.din("b_sl", [128, 2, 64, 16])
        self.iota_d = self.din("iotaab", [128, 48])
        self.maskp_d = self.din("maskp", [128, 4, 2])
        self.y = self.nc.dram_tensor("y", [S, D], F32, kind="ExternalOutput").ap()
        self.wscr = {n: self.dscr("s_" + n, [128, c // 128, k // 128, 128], BF16) for n, (k, c) in self.wdefs.items()}
        self.x1 = self.dscr("x1", [S, D], F32)
        self.uS = self.dscr("uS", [8, 128, S], BF16)
        self.kS = self.dscr("kS", [2, 128, S], BF16)
        self.vS = self.dscr("vS", [S, 256], BF16)
        self.ysS = self.dscr("ysS", [8, 128, S], BF16)
        self.pwS = self.dscr("pwS", [128, 8, 2, 1024], BF16)
        self.ps = self.nc.alloc_psum_tensor("ps", [128, 8, 512], F32).ap()
        R = Region(self, 16640, "c")
        self.ident = R.alloc("ident", [128, 128], BF16)
        self.onesm = R.alloc("onesm", [128, 128], BF16)
        self.rotm = R.alloc("rotm", [128, 128], F32)
        self.gT3 = R.alloc("gT3", [128, 3, 16], F32)
        self.small = R.alloc("small", [128, 32], F32)
        self.esink = R.alloc("esink", [128, 8], F32)
        self.tri = R.alloc("tri", [128, 2, 128], BF16)
        self.valid = R.alloc("valid", [128, self.NB], F32)
        self.epsc = R.alloc("epsc", [128, 8], F32)
        self.op("dve", lambda e: e.memset(self.epsc, EPS), writes=["epsc"])
        self.cbase = (R.off + 63) // 64 * 64
        for nm, dst, src in (("ident", self.ident, self.ident_d), ("onesm", self.onesm, self.ones_d),
                             ("rotm", self.rotm, self.rotm_d), ("gT3", self.gT3, self.gT3_d),
                             ("small", self.small, self.small_d), ("tri", self.tri, self.tri_d),
                             ("valid", self.valid, self.valid_d)):
            self.dma("sp", dst, src, "c_" + nm, writes=["c_" + nm])
        self.op("act", lambda e: e.activation(out=self.esink, in_=self.small[:, 2:10], func=AF.Exp),
                reads=["c_small"], writes=["esink"])
        self.wcount = 0
        if "P" in self.phases:
            self.precast()
        if "A" in self.phases:
            self.phaseAB("A")
        if "S" in self.phases:
            self.sc.barrier()
            self.phaseS()
        if "B" in self.phases:
            self.sc.barrier()
            self.phaseAB("B")
        self.sc.emit(final_wait_keys=self.outkeys)

    def wchunks(self, n):
        K, C = self.wdefs[n]
        nk = K // 128
        if n in ("f1gu", "f2gu"):
            g = [(0, 2048), (2048, 4096), (4096, 5632)]
            return [(x[0], x[1], 0, nk) for i in range(3) for x in (g[i], (g[i][0] + 5632, g[i][1] + 5632))]
        if n in ("f1dn", "f2dn"):
            return [(0, C, 0, 22), (0, C, 22, 44)]
        return [(c0, min(C, c0 + 2048), 0, nk) for c0 in range(0, C, 2048)]

    def wk(self, n, ct, kc=0):
        c = ct * 128
        for i, (c0, c1, k0, k1) in enumerate(self.wchunks(n)):
            if c0 <= c < c1 and k0 <= kc < k1:
                return "ws_%s_%d" % (n, i)
        raise KeyError((n, ct, kc))

    def precast(self):
        def cast(n, idxs=None):
            src, dst = self.wsrc[n], self.wscr[n]
            for i, (c0, c1, k0, k1) in enumerate(self.wchunks(n)):
                if idxs is not None and i not in idxs:
                    continue
                key = "ws_%s_%d" % (n, i)
                for kc in range(k0, k1):
                    s = src[kc * 128:(kc + 1) * 128, c0:c1].rearrange("p (ct j) -> p ct j", j=128)
                    d = dst[:, c0 // 128:c1 // 128, kc, :]
                    self.dma("pool", d, s, key, pwrites=[key])
        cast("f1gu", [0, 1, 2, 3])
        cast("f1dn", [0])
        cast("f1gu", [4, 5])
        cast("f1dn", [1])
        for n in ["win", "wglu", "wao", "wso", "wo"]:
            cast(n)
        cast("f2gu", [0, 1, 2, 3])
        cast("f2dn", [0])
        cast("f2gu", [4, 5])
        cast("f2dn", [1])

    def mm(self, bank, pairs, reads, start=True, stop=True, out=None):
        out_ap = self.ps[:, bank, :] if out is None else out

        def fn(e, pairs=pairs, out_ap=out_ap, start=start, stop=stop):
            n = len(pairs)
            for i, (l, r) in enumerate(pairs):
                ins = e.matmul(out_ap, lhsT=l, rhs=r, start=(start and i == 0), stop=(stop and i == n - 1))
            return ins
        return self.op("pe", fn, reads=reads, writes=["ps%d" % bank])

    def tt(self, eng, out, in0, in1, op, reads, writes=(), pwrites=()):
        return self.op(eng, lambda e: e.tensor_tensor(out=out, in0=in0, in1=in1, op=op), reads=reads, writes=writes, pwrites=pwrites)

    def ts(self, eng, out, in0, s1, op0, reads, writes=(), pwrites=(), s2=None, op1=None):
        if op1 is None:
            fn = lambda e: e.tensor_scalar(out=out, in0=in0, scalar1=s1, scalar2=None, op0=op0)
        else:
            fn = lambda e: e.tensor_scalar(out=out, in0=in0, scalar1=s1, scalar2=s2, op0=op0, op1=op1)
        return self.op(eng, fn, reads=reads, writes=writes, pwrites=pwrites)

    def stt(self, out, in0, scalar, in1, op0, op1, reads, writes=(), pwrites=()):
        return self.op("dve", lambda e: e.scalar_tensor_tensor(out=out, in0=in0, scalar=scalar, in1=in1, op0=op0, op1=op1),
                       reads=reads, writes=writes, pwrites=pwrites)

    def actf(self, out, in_, func, reads, writes=(), pwrites=(), scale=1.0, bias=None):
        if bias is None:
            fn = lambda e: e.activation(out=out, in_=in_, func=func, scale=scale)
        else:
            fn = lambda e: e.activation(out=out, in_=in_, func=func, scale=scale, bias=bias)
        return self.op("act", fn, reads=reads, writes=writes, pwrites=pwrites)

    def alloc_AB(self, ph):
        R = Region(self, self.cbase, "ab" + ph)
        self.xt = R.alloc("xt", [128, 4, D], F32)
        self.xn = R.alloc("xn", [128, 2, D], BF16)
        self.ss4 = R.alloc("ss4", [128, 4], F32)
        self.rs4 = R.alloc("rs4", [128, 4], F32)
        self.Hbuf = [(R.alloc("hT", [128, 16, 512], BF16), "hT0"), (R.alloc("hT2", [128, 16, 512], BF16), "hT1")]
        self.ssp = R.alloc("ssp", [128, 4], F32)
        self.rsp = R.alloc("rsp", [128, 4], F32)
        self.use_h(0)
        self.actT = R.alloc("actT", [128, 22, 512], BF16)
        self.mergedT = self.actT[:, 0:16, :]
        self.sg = R.alloc("sg", [128, 2, 512], F32)
        self.wring = [R.alloc("wr%d" % i, [128, WSLOT // 2], BF16) for i in range(NWS)]
        self.cst = R.alloc("cst", [128, 2, 512], F32)
        self.TT = R.alloc("TT", [128, 4096], F32)
        self.TTb = self.TT.bitcast(BF16)
        self.T = [self.TT[:, i * 512:(i + 1) * 512] for i in range(8)]
        if ph == "A":
            self.uT = R.alloc("uT", [128, 8, 512], BF16)
            self.kT = R.alloc("kT", [128, 2, 512], BF16)
            self.vx = R.alloc("vx", [128, 4, 256], BF16)
        else:
            self.qT = R.alloc("qT", [128, 8, 512], BF16)
            self.kwin = R.alloc("kwin", [128, 2, 768], BF16)
            self.vwin = R.alloc("vwin", [128, 6, 256], BF16)
            self.vrw = R.alloc("vrw", [128, 6, 128], BF16)
            self.attnT = R.alloc("attnT", [128, 8, 512], BF16)
            self.zT = R.alloc("zT", [128, 8, 512], BF16)
            self.z2T = R.alloc("z2T", [128, 8, 512], BF16)

    def use_h(self, i):
        self.hT, self.hk = self.Hbuf[i]

    def prenorm_a(self, rows_ap, b):
        tk = ["T0", "T1", "T2", "T3"]
        xs = self.TT[:, 0:2048]
        self.dma("sp", xs, rows_ap, "xs", writes=tk)
        xb, xk = self.xn[:, b % 2, :], "xn%d" % (b % 2)
        self.actf(xb, xs, AF.Square, reads=tk, writes=[xk])
        sspb, rspb = self.ssp[:, b:b + 1], self.rsp[:, b:b + 1]
        self.op("dve", lambda e: e.tensor_reduce(out=sspb, in_=xb, axis=AX.X, op=ALU.add),
                reads=[xk], writes=["ssp%d" % b])
        self.actf(self.rsp[:, b:b + 1], self.ssp[:, b:b + 1], AF.Sqrt, reads=["ssp%d" % b, "epsc"], writes=["rsp%d" % b],
                  scale=1.0 / D, bias=self.epsc[:, 0:1])
        self.op("dve", lambda e: e.reciprocal(out=rspb, in_=rspb),
                reads=["rsp%d" % b], writes=["rsp%d" % b])
        self.actf(xb, xs, AF.Copy, reads=tk + ["rsp%d" % b], writes=[xk], scale=self.rsp[:, b:b + 1])

    def prenorm_b(self, b, gi, hi):
        H, hk = self.Hbuf[hi]
        xb, xk = self.xn[:, b % 2, :], "xn%d" % (b % 2)
        b0 = self.bank()
        if b0 % 2 == 1:
            b0 = self.bank()
        self.bank()
        pst = [self.ps[:, b0, :].bitcast(BF16), self.ps[:, b0 + 1, :].bitcast(BF16)]

        def fn(e, xb=xb, pst=pst):
            for c in range(16):
                ins = e.transpose(out=pst[c // 8][:, (c % 8) * 128:(c % 8 + 1) * 128], in_=xb[:, c * 128:(c + 1) * 128],
                                  identity=self.ident)
            return ins
        self.op("pe", fn, reads=[xk, "c_ident"], writes=["ps%d" % b0, "ps%d" % (b0 + 1)])
        for hb in range(2):
            self.tt("dve", H[:, hb * 8:hb * 8 + 8, b * 128:(b + 1) * 128],
                    pst[hb].rearrange("p (c t) -> p c t", t=128),
                    self.gT3[:, gi, hb * 8:hb * 8 + 8].unsqueeze(2).to_broadcast([128, 8, 128]), ALU.mult,
                    reads=["ps%d" % (b0 + hb), "c_gT3"], pwrites=[hk])

    def norm_hT(self, gi):
        for b in range(4):
            xb = self.xn[:, b % 2, :]
            self.actf(xb, self.xt[:, b, :], AF.Square, reads=["xt%d" % b], writes=["xn%d" % (b % 2)])
            ss4b = self.ss4[:, b:b + 1]
            self.op("dve", lambda e, ss4b=ss4b, xb=xb: e.tensor_reduce(out=ss4b, in_=xb, axis=AX.X, op=ALU.add),
                    reads=["xn%d" % (b % 2)], pwrites=["ss4"])
        self.actf(self.rs4, self.ss4, AF.Sqrt, reads=["ss4", "epsc"], writes=["rs4"], scale=1.0 / D, bias=self.epsc[:, 0:1])
        rs4 = self.rs4
        self.op("dve", lambda e: e.reciprocal(out=rs4, in_=rs4), reads=["rs4"], writes=["rs4"])
        for b in range(4):
            xb = self.xn[:, b % 2, :]
            self.actf(xb, self.xt[:, b, :], AF.Copy, reads=["xt%d" % b, "rs4"], writes=["xn%d" % (b % 2)],
                      scale=self.rs4[:, b:b + 1])
            b0 = self.bank()
            if b0 % 2 == 1:
                b0 = self.bank()
            self.bank()
            pst = [self.ps[:, b0, :].bitcast(BF16), self.ps[:, b0 + 1, :].bitcast(BF16)]

            def fn(e, xb=xb, pst=pst):
                for c in range(16):
                    ins = e.transpose(out=pst[c // 8][:, (c % 8) * 128:(c % 8 + 1) * 128], in_=xb[:, c * 128:(c + 1) * 128],
                                      identity=self.ident)
                return ins
            self.op("pe", fn, reads=["xn%d" % (b % 2), "c_ident"], writes=["ps%d" % b0, "ps%d" % (b0 + 1)])
            for hb in range(2):
                self.tt("dve", self.hT[:, hb * 8:hb * 8 + 8, b * 128:(b + 1) * 128],
                        pst[hb].rearrange("p (c t) -> p c t", t=128),
                        self.gT3[:, gi, hb * 8:hb * 8 + 8].unsqueeze(2).to_broadcast([128, 8, 128]), ALU.mult,
                        reads=["ps%d" % (b0 + hb), "c_gT3"], pwrites=[self.hk])

    def ffn(self, wgu, wdn, store=None, hooks=None):
        hooks = hooks or {}
        descs = []
        for half in range(2):
            for jj in range(22):
                j = half * 22 + jj
                descs.append(([128, 2, 16, 128],
                              [(lambda v: v[:, 0], self.wscr[wgu][:, j], self.wk(wgu, j)),
                               (lambda v: v[:, 1], self.wscr[wgu][:, 44 + j], self.wk(wgu, 44 + j))]))
            for s in range(4):
                for kh in range(2):
                    k0 = half * 22 + kh * 11
                    descs.append(([128, 4, 11, 128],
                                  [(lambda v: v, self.wscr[wdn][:, 4 * s:4 * s + 4, k0:k0 + 11, :], self.wk(wdn, 4 * s, k0))]))
        get = self.wstream(descs)
        di = 0
        for half in range(2):
            for jj in range(22):
                w, wk = get(di)
                di += 1
                bg, bu = self.bank(), self.bank()
                self.mm(bg, [(w[:, 0, k, :], self.hT[:, k, :]) for k in range(16)], reads=[wk, self.hk])
                self.mm(bu, [(w[:, 1, k, :], self.hT[:, k, :]) for k in range(16)], reads=[wk, self.hk])
                si = jj % 2
                self.actf(self.sg[:, si, :], self.ps[:, bg, :], AF.Silu, reads=["ps%d" % bg], writes=["sg%d" % si])
                self.tt("dve", self.actT[:, jj, :], self.sg[:, si, :], self.ps[:, bu, :], ALU.mult,
                        reads=["sg%d" % si, "ps%d" % bu], writes=["act%d" % jj])
                if ("gu", half, jj) in hooks:
                    hooks[("gu", half, jj)]()
            for s in range(4):
                if ("dn", half, s) in hooks:
                    hooks[("dn", half, s)]()
                banks = [self.bank() for _ in range(4)]
                for kh in range(2):
                    w, wk = get(di)
                    di += 1
                    for tb in range(4):
                        pairs = [(self.actT[:, kh * 11 + j2, tb * 128:(tb + 1) * 128], w[:, :, j2, :]) for j2 in range(11)]
                        self.mm(banks[tb], pairs, reads=[wk] + ["act%d" % (kh * 11 + j2) for j2 in range(11)],
                                start=(kh == 0), stop=(kh == 1))
                for tb in range(4):
                    sl = self.xt[:, tb, s * 512:(s + 1) * 512]
                    self.stt(sl, self.ps[:, banks[tb], :], 0.5, sl, ALU.mult, ALU.add,
                             reads=["ps%d" % banks[tb], "xt%d" % tb], pwrites=["xt%d" % tb])
                    if store is not None and half == 1:
                        store(tb, s)

    def qk_head(self, bank_in, gcol, out_ap, okey, si, e2="dve", split=False):
        T = self.T
        sq, rv, qn, t1 = self.TTb[:, 4 * si * 1024:4 * si * 1024 + 512], T[4 * si + 1], T[4 * si + 2], T[4 * si + 3]
        ks, kr, kq, k1 = ["T%d" % (4 * si + i) for i in range(4)]
        pin = self.ps[:, bank_in, :]
        self.actf(sq, pin, AF.Square, reads=["ps%d" % bank_in], writes=[ks])
        b2 = self.bank()
        self.mm(b2, [(self.onesm, sq)], reads=["c_onesm", ks])
        self.actf(rv, self.ps[:, b2, :], AF.Ln, reads=["ps%d" % b2, "epsc"], writes=[kr], scale=1.0 / 128, bias=self.epsc[:, 0:1])
        self.actf(rv, rv, AF.Exp, reads=[kr], writes=[kr], scale=-0.5)
        self.stt(qn, pin, self.small[:, gcol:gcol + 1], rv, ALU.mult, ALU.mult, reads=["ps%d" % bank_in, kr, "c_small"], writes=[kq])
        b3 = self.bank()
        self.mm(b3, [(self.rotm, qn)], reads=["c_rotm", kq])

        def stage_b():
            self.tt(e2, t1, qn, self.cst[:, 0, :], ALU.mult, reads=[kq, "cst"], writes=[k1])
            self.tt("dve", rv, self.ps[:, b3, :], self.cst[:, 1, :], ALU.mult, reads=["ps%d" % b3, "cst"], writes=[kr])
            self.tt(e2, out_ap, t1, rv, ALU.add, reads=[k1, kr], pwrites=[okey])
        if split:
            return stage_b
        stage_b()

    def phaseAB(self, ph):
        S, NT = self.S, self.NT
        self.alloc_AB(ph)
        xsrc = self.x if ph == "A" else self.x1
        xsk = "x" if ph == "A" else "x1"
        g_pre = 0 if ph == "A" else 1

        def preA(tn, b):
            rr = tn * 512 + b * 128
            self.prenorm_a(xsrc[rr:rr + 128, :], b)

        def preB(b):
            self.prenorm_b(b, g_pre, 0)
        for b in range(4):
            preA(0, b)
            preB(b)
        for t in range(NT):
            r0 = t * 512
            def xtload(r0=r0):
                for b in range(4):
                    self.dma("sp", self.xt[:, b, :], xsrc[r0 + b * 128:r0 + (b + 1) * 128, :], "xt%d" % b, reads=[xsk], writes=["xt%d" % b])
            if ph == "A":
                xtload()
            self.dma("sp", self.cst[:, 0, :], self.cos_d[:, r0:r0 + 512], "cst", pwrites=["cst"])
            self.dma("sp", self.cst[:, 1, :], self.sin_d[:, r0:r0 + 512], "cst", pwrites=["cst"])
            nxt = t + 1 < NT
            if ph == "A":
                def st1(tb, s, r0=r0):
                    self.dma("act", self.x1[r0 + tb * 128:r0 + (tb + 1) * 128, s * 512:(s + 1) * 512],
                             self.xt[:, tb, s * 512:(s + 1) * 512], "x1_%d" % tb, reads=["xt%d" % tb], pwrites=["x1"])
                hooks = {}
                if nxt:
                    hooks[("gu", 1, 8)] = (lambda t=t: preA(t + 1, 0))
                    hooks[("gu", 1, 16)] = (lambda t=t: preA(t + 1, 1))
                    hooks[("dn", 1, 0)] = (lambda t=t: preB(0))
                    hooks[("dn", 1, 1)] = (lambda t=t: (preB(1), preA(t + 1, 2)))
                    hooks[("dn", 1, 2)] = (lambda t=t: preA(t + 1, 3))
                    hooks[("dn", 1, 3)] = (lambda t=t: preB(2))
                self.use_h(0)
                self.ffn("f1gu", "f1dn", store=st1, hooks=hooks)
                if nxt:
                    preB(3)
                self.use_h(1)
                self.norm_hT(1)
                self.proj_kvu(t)
            else:
                self.use_h(0)
                self.mixer(t, after_q=xtload)
                self.use_h(1)
                self.norm_hT(2)

                def st2(tb, s, r0=r0):
                    self.dma("act", self.y[r0 + tb * 128:r0 + (tb + 1) * 128, s * 512:(s + 1) * 512],
                             self.xt[:, tb, s * 512:(s + 1) * 512], "y_%d" % tb, reads=["xt%d" % tb], pwrites=["y"])
                hooks = {}
                if nxt:
                    for b, (pa, pb) in enumerate([(("gu", 0, 2), ("gu", 0, 7)), (("gu", 0, 11), ("gu", 0, 16)),
                                                  (("gu", 1, 2), ("gu", 1, 7)), (("gu", 1, 11), ("gu", 1, 16))]):
                        hooks[pa] = (lambda t=t, b=b: preA(t + 1, b))
                        hooks[pb] = (lambda b=b: preB(b))
                self.ffn("f2gu", "f2dn", store=st2, hooks=hooks)
        if ph == "B":
            self.outkeys.append("y")

    def proj_kvu(self, t):
        r0 = t * 512
        win = self.wscr["win"]
        descs = []
        for c in range(8):
            descs.append(([128, 16, 128], [(lambda v: v, win[:, 12 + c], self.wk("win", 12 + c))]))
        for g in range(2):
            descs.append(([128, 16, 128], [(lambda v: v, win[:, 8 + g], self.wk("win", 8 + g))]))
        descs.append(([128, 2, 16, 128], [(lambda v: v, win[:, 10:12], self.wk("win", 10))]))
        get = self.wstream(descs)
        for c in range(8):
            w, wk = get(c)
            b = self.bank()
            self.mm(b, [(w[:, k, :], self.hT[:, k, :]) for k in range(16)], reads=[wk, self.hk])
            self.actf(self.uT[:, c, :], self.ps[:, b, :], AF.Copy, reads=["ps%d" % b], writes=["uT%d" % c])
            self.dma("act", self.uS[c][:, r0:r0 + 512], self.uT[:, c, :], "uS%d" % c, reads=["uT%d" % c], pwrites=["uS"])
        for g in range(2):
            w, wk = get(8 + g)
            b = self.bank()
            self.mm(b, [(w[:, k, :], self.hT[:, k, :]) for k in range(16)], reads=[wk, self.hk])
            self.qk_head(b, 1, self.kT[:, g, :], "kT%d" % g, g % 2)
            self.dma("act", self.kS[g][:, r0:r0 + 512], self.kT[:, g, :], "kS%d" % g, reads=["kT%d" % g], pwrites=["kS"])
        w, wk = get(10)
        for tb in range(4):
            b = self.bank()
            self.mm(b, [(self.hT[:, k, tb * 128:(tb + 1) * 128], w[:, :, k, :]) for k in range(16)], reads=[wk, self.hk],
                    out=self.ps[:, b, 0:256])
            gb = t * 4 + tb
            self.ts("dve", self.vx[:, tb, :], self.ps[:, b, 0:256], self.valid[:, gb:gb + 1], ALU.mult,
                    reads=["ps%d" % b, "c_valid"], writes=["vx%d" % tb])
            self.dma("act", self.vS[r0 + tb * 128:r0 + (tb + 1) * 128, :], self.vx[:, tb, :], "vS%d" % tb, reads=["vx%d" % tb], pwrites=["vS"])

    def mixer(self, t, after_q=None):
        S, NB = self.S, self.NB
        r0 = t * 512
        win = self.wscr["win"]
        lo, hi = max(0, r0 - 128), min(S, r0 + 640)
        off = lo - (r0 - 128)
        nb_lo, nb_n = off // 128, (hi - lo) // 128
        for g in range(2):
            self.dma("sp", self.kwin[:, g, off:off + hi - lo], self.kS[g][:, lo:hi], "kwin", reads=["kS"], pwrites=["kwin"])
        self.dma("sp", self.vwin[:, nb_lo:nb_lo + nb_n, :], self.vS[lo:hi, :].rearrange("(b p) c -> p b c", p=128), "vwin",
                 reads=["vS"], pwrites=["vwin"])
        self.dma("sp", self.vrw[:, nb_lo:nb_lo + nb_n, :], self.vrep_d[lo:hi, :].rearrange("(b p) c -> p b c", p=128), "vrw",
                 pwrites=["vrw"])
        for o in range(8):
            self.dma("sp", self.z2T[:, o, :], self.ysS[o][:, r0:r0 + 512], "yld%d" % o, reads=["ysS"], pwrites=["z2T%d" % o])
        descs = []
        for h in range(8):
            descs.append(([128, 16, 128], [(lambda v: v, win[:, h], self.wk("win", h))]))
        for c in range(8):
            descs.append(([128, 8, 128], [(lambda v: v, self.wscr["wglu"][:, c], self.wk("wglu", c))]))
        for f in range(16):
            descs.append(([128, 48, 128], [
                (lambda v: v[:, 0:8], self.wscr["wao"][:, f], self.wk("wao", f)),
                (lambda v: v[:, 8:16], self.wscr["wso"][:, f], self.wk("wso", f)),
                (lambda v: v[:, 16:32], win[:, 20 + f], self.wk("win", 20 + f)),
                (lambda v: v[:, 32:48], win[:, 36 + f], self.wk("win", 36 + f))]))
        for s in range(4):
            for kh in range(2):
                descs.append(([128, 4, 8, 128], [(lambda v: v, self.wscr["wo"][:, 4 * s:4 * s + 4, kh * 8:kh * 8 + 8, :], self.wk("wo", 4 * s))]))
        get = self.wstream(descs)
        di = 0
        T = self.T
        zkeys = ["zT%d" % o for o in range(8)]

        def gelu_a(o):
            xv = self.z2T[:, o, :]
            ta, tb_ = self.sg[:, 0, :], self.sg[:, 1, :]
            self.tt("pool", ta, xv, xv, ALU.mult, reads=["z2T%d" % o], writes=["sg0"])
            self.ts("pool", tb_, ta, 0.044715, ALU.mult, reads=["sg0"], writes=["sg1"], s2=1.0, op1=ALU.add)
            self.tt("pool", self.zT[:, o, :], tb_, xv, ALU.mult, reads=["sg1", "z2T%d" % o], writes=["zT%d" % o])

        def gelu_b(o):
            xv = self.z2T[:, o, :]
            tb_, kb_ = self.sg[:, o % 2, :], "sg%d" % (o % 2)
            self.actf(tb_, self.zT[:, o, :], AF.Sigmoid, reads=["zT%d" % o], writes=[kb_], scale=1.5957691216057308)
            self.tt("pool", self.zT[:, o, :], xv, tb_, ALU.mult, reads=[kb_, "z2T%d" % o], writes=["zT%d" % o])

        def glu(c, di):
            w, wk = get(di)
            b = self.bank()
            self.mm(b, [(w[:, k, :], self.zT[:, k, :]) for k in range(8)], reads=[wk] + zkeys)
            sgl, ksg = self.sg[:, c % 2, :], "sg%d" % (c % 2)
            self.actf(sgl, self.ps[:, b, :], AF.Sigmoid, reads=["ps%d" % b, "c_small"], writes=[ksg], bias=self.small[:, 10 + c:11 + c])
            self.tt("pool", self.z2T[:, c, :], self.zT[:, c, :], sgl, ALU.mult, reads=[ksg, "zT%d" % c], writes=["z2T%d" % c])

        def qproj(h, di):
            w, wk = get(di)
            b = self.bank()
            self.mm(b, [(w[:, k, :], self.hT[:, k, :]) for k in range(16)], reads=[wk, self.hk])
            return b
        for o in range(8):
            gelu_a(o)
        qb_next = qproj(0, di)
        di += 1
        pend = None
        for h in range(8):
            qb_cur = qb_next
            if h + 1 < 8:
                qb_next = qproj(h + 1, di)
                di += 1
            sb_ = self.qk_head(qb_cur, 0, self.qT[:, h, :], "qT", h % 2, e2="dve", split=True)
            if pend is not None:
                pend()
            pend = sb_
        pend()
        if after_q is not None:
            after_q()
        sets = [(self.TTb[:, 0:3072].rearrange("p (g i n) -> p g i n", g=2, i=3), ["T0", "T1", "T2"]),
                (self.TTb[:, 3072:6144].rearrange("p (g i n) -> p g i n", g=2, i=3), ["T3", "T4", "T5"])]
        dn = self.TT[:, 3072:4096]
        dkeys = ["T6", "T7"]

        def evenpair():
            b0 = self.bank()
            if b0 % 2 == 1:
                b0 = self.bank()
            self.bank()
            return b0

        def s_stage(qb):
            gq = t * 4 + qb
            ds = [d for d in (-1, 0, 1) if 0 <= gq + d < NB]
            ex, ekeys = sets[qb % 2]
            for g in range(2):
                qv = self.qT[:, 4 * g:4 * g + 4, qb * 128:(qb + 1) * 128]
                for idx, d in enumerate(ds):
                    wb = qb + 1 + d
                    bS = self.bank()
                    self.mm(bS, [(self.kwin[:, g, wb * 128:(wb + 1) * 128], qv)], reads=["kwin", "qT"])
                    self.actf(ex[:, g, idx, :], self.ps[:, bS, :], AF.Exp, reads=["ps%d" % bS], pwrites=ekeys,
                              scale=float(128 ** -0.5))
            for idx, d in enumerate(ds):
                if d != 0:
                    ev = ex[:, :, idx, :].rearrange("p g (h q) -> p g h q", q=128)
                    mk = self.tri[:, 0 if d == -1 else 1, :].unsqueeze(1).unsqueeze(1).to_broadcast([128, 2, 4, 128])
                    self.tt("dve", ev, ev, mk, ALU.mult, reads=ekeys + ["c_tri"], pwrites=ekeys)

        def pv_stage(qb):
            gq = t * 4 + qb
            ds = [d for d in (-1, 0, 1) if 0 <= gq + d < NB]
            ex, ekeys = sets[qb % 2]
            bO = evenpair()
            bD = evenpair()
            for g in range(2):
                self.mm(bO + g, [(self.vwin[:, qb + 1 + d, g * 128:(g + 1) * 128], ex[:, g, idx, :]) for idx, d in enumerate(ds)],
                        reads=["vwin"] + ekeys)
                self.mm(bD + g, [(self.vrw[:, qb + 1 + d, :], ex[:, g, idx, :]) for idx, d in enumerate(ds)], reads=["vrw"] + ekeys)
            dv = dn.rearrange("p (g h q) -> p g h q", g=2, h=4)
            self.tt("dve", dv, self.ps[:, bD:bD + 2, :].rearrange("p g (h q) -> p g h q", q=128),
                    self.esink[:, 0:8].rearrange("p (g h) -> p g h", g=2).unsqueeze(3).to_broadcast([128, 2, 4, 128]), ALU.add,
                    reads=["ps%d" % bD, "ps%d" % (bD + 1), "esink"], writes=dkeys)
            self.op("dve", lambda e: e.reciprocal(out=dn, in_=dn), reads=dkeys, writes=dkeys)
            self.tt("dve", self.attnT[:, :, qb * 128:(qb + 1) * 128].rearrange("p (g h) q -> p g h q", g=2),
                    self.ps[:, bO:bO + 2, :].rearrange("p g (h q) -> p g h q", q=128), dv,
                    ALU.mult, reads=["ps%d" % bO, "ps%d" % (bO + 1)] + dkeys, pwrites=["attnT"])

        s_stage(0)
        for qb in range(4):
            if qb + 1 < 4:
                s_stage(qb + 1)
            pv_stage(qb)
        for o in range(8):
            gelu_b(o)
        for c in range(8):
            glu(c, di)
            di += 1
        z2keys = ["z2T%d" % o for o in range(8)]
        for f in range(16):
            w, wk = get(di)
            di += 1
            bA, bB, bG, bH = self.bank(), self.bank(), self.bank(), self.bank()
            self.mm(bG, [(w[:, 16 + k, :], self.hT[:, k, :]) for k in range(16)], reads=[wk, self.hk])
            self.mm(bH, [(w[:, 32 + k, :], self.hT[:, k, :]) for k in range(16)], reads=[wk, self.hk])
            self.mm(bA, [(w[:, k, :], self.attnT[:, k, :]) for k in range(8)], reads=[wk, "attnT"])
            self.mm(bB, [(w[:, 8 + k, :], self.z2T[:, k, :]) for k in range(8)], reads=[wk] + z2keys)
            s4 = (f % 2) * 4
            sa, ss_, m1, m2 = T[s4], T[s4 + 1], T[s4 + 2], T[s4 + 3]
            ka, ks_, k1, k2 = ["T%d" % (s4 + i) for i in range(4)]
            self.actf(sa, self.ps[:, bG, :], AF.Sigmoid, reads=["ps%d" % bG], writes=[ka])
            self.actf(ss_, self.ps[:, bH, :], AF.Sigmoid, reads=["ps%d" % bH], writes=[ks_])
            self.tt("dve", m1, sa, self.ps[:, bA, :], ALU.mult, reads=[ka, "ps%d" % bA], writes=[k1])
            self.tt("dve", m2, ss_, self.ps[:, bB, :], ALU.mult, reads=[ks_, "ps%d" % bB], writes=[k2])
            self.tt("pool", self.mergedT[:, f, :], m1, m2, ALU.add, reads=[k1, k2], writes=["act%d" % f])
        for s in range(4):
            banks = [self.bank() for _ in range(4)]
            for kh in range(2):
                w, wk = get(di)
                di += 1
                for tb in range(4):
                    pairs = [(self.mergedT[:, kh * 8 + j2, tb * 128:(tb + 1) * 128], w[:, :, j2, :]) for j2 in range(8)]
                    self.mm(banks[tb], pairs, reads=[wk] + ["act%d" % (kh * 8 + j2) for j2 in range(8)],
                            start=(kh == 0), stop=(kh == 1))
            for tb in range(4):
                sl = self.xt[:, tb, s * 512:(s + 1) * 512]
                self.tt("dve", sl, self.ps[:, banks[tb], :], sl, ALU.add, reads=["ps%d" % banks[tb], "xt%d" % tb],
                        pwrites=["xt%d" % tb])

    def frac(self, out, a, tmp, shift, rk, wk, tk):
        if shift != 0.0:
            self.ts("dve", out, a, shift, ALU.add, reads=[rk], writes=[wk])
            src, sk = out, wk
        else:
            src, sk = a, rk
        self.ts("dve", tmp, src, MAGIC, ALU.add, reads=[sk], writes=[tk])
        self.ts("dve", tmp, tmp, -MAGIC, ALU.add, reads=[tk], writes=[tk])
        self.tt("dve", out, src, tmp, ALU.subtract, reads=[sk, tk], writes=[wk])

    def cmul(self, o_re, o_im, a_re, a_im, b_re, b_im, t1, t2, reads, ok, tkeys, nim=None):
        k1, k2 = tkeys
        self.tt("dve", t1, a_re, b_re, ALU.mult, reads=reads, writes=[k1])
        self.tt("dve", t2, a_im, b_im, ALU.mult, reads=reads, writes=[k2])
        self.tt("dve", o_re, t1, t2, ALU.subtract, reads=[k1, k2], pwrites=[ok])
        self.tt("dve", t1, a_re, b_im, ALU.mult, reads=reads, writes=[k1])
        self.tt("dve", t2, a_im, b_re, ALU.mult, reads=reads, writes=[k2])
        self.tt("dve", o_im, t1, t2, ALU.add, reads=[k1, k2], pwrites=[ok])
        if nim is not None:
            self.ts("dve", nim, o_im, -1.0, ALU.mult, reads=[ok], pwrites=[ok])

    def cexp(self, R, n, lp, tag):
        A = lambda nm: R.alloc(tag + nm, [128, n], F32)
        lre, lim, ldt = lp[:, 0, :], lp[:, 1, :], lp[:, 2, :]
        ik = tag + "in"
        dt, t, mag, y, r1, r2, sn, cs = [A(x) for x in ("dt", "t", "mag", "y", "r1", "r2", "sn", "cs")]
        lbr, lbi, nr, den, fr, fi, u1, u2 = [A(x) for x in ("lbr", "lbi", "nr", "den", "fr", "fi", "u1", "u2")]
        K = lambda x: tag + x
        self.actf(dt, ldt, AF.Exp, reads=[ik], writes=[K("dt")])
        self.tt("dve", t, lre, dt, ALU.mult, reads=[ik, K("dt")], writes=[K("t")])
        self.actf(mag, t, AF.Exp, reads=[K("t")], writes=[K("mag")])
        self.stt(y, lim, 1.0 / TWO_PI, dt, ALU.mult, ALU.mult, reads=[ik, K("dt")], writes=[K("y")])
        self.frac(r1, y, r2, 0.0, K("y"), K("r1"), K("r2"))
        self.actf(sn, r1, AF.Sin, reads=[K("r1")], writes=[K("sn")], scale=TWO_PI)
        self.frac(r1, y, r2, 0.25, K("y"), K("r1"), K("r2"))
        self.actf(cs, r1, AF.Sin, reads=[K("r1")], writes=[K("cs")], scale=TWO_PI)
        self.tt("dve", lbr, mag, cs, ALU.mult, reads=[K("mag"), K("cs")], writes=[K("lbr")])
        self.tt("dve", lbi, mag, sn, ALU.mult, reads=[K("mag"), K("sn")], writes=[K("lbi")])
        self.ts("dve", nr, lbr, -1.0, ALU.add, reads=[K("lbr")], writes=[K("nr")])
        self.tt("dve", u1, lre, lre, ALU.mult, reads=[ik], writes=[K("u1")])
        self.tt("dve", u2, lim, lim, ALU.mult, reads=[ik], writes=[K("u2")])
        self.tt("dve", den, u1, u2, ALU.add, reads=[K("u1"), K("u2")], writes=[K("den")])
        self.op("dve", lambda e: e.reciprocal(out=den, in_=den), reads=[K("den")], writes=[K("den")])
        self.tt("dve", u1, nr, lre, ALU.mult, reads=[K("nr"), ik], writes=[K("u1")])
        self.tt("dve", u2, lbi, lim, ALU.mult, reads=[K("lbi"), ik], writes=[K("u2")])
        self.tt("dve", u1, u1, u2, ALU.add, reads=[K("u1"), K("u2")], writes=[K("u1")])
        self.tt("dve", fr, u1, den, ALU.mult, reads=[K("u1"), K("den")], writes=[K("fr")])
        self.tt("dve", u1, lbi, lre, ALU.mult, reads=[K("lbi"), ik], writes=[K("u1")])
        self.tt("dve", u2, nr, lim, ALU.mult, reads=[K("nr"), ik], writes=[K("u2")])
        self.tt("dve", u1, u1, u2, ALU.subtract, reads=[K("u1"), K("u2")], writes=[K("u1")])
        self.tt("dve", fi, u1, den, ALU.mult, reads=[K("u1"), K("den")], writes=[K("fi")])
        return dict(lbr=lbr, lbi=lbi, fr=fr, fi=fi, t=t, y=y, u1=u1, u2=u2, r1=r1, r2=r2,
                    keys=[K(x) for x in ("lbr", "lbi", "fr", "fi", "t", "y")])

    def phaseS(self):
        S = self.S
        N = S // 8
        NA = N // 16
        R = Region(self, self.cbase, "s")
        csl = R.alloc("c_sl", [128, 2, 64, 16], F32)
        maskp = R.alloc("maskp", [128, 4, 2], F32)
        pw = R.alloc("pw", [128, 9, 3, 64], F32)
        r8 = R.alloc("r8", [128, 64], F32)
        EA = R.alloc("EA", [128, 2, 64, 32], F32)
        EB = R.alloc("EB", [128, 2, 64, 16], F32)
        FB = R.alloc("FB", [128, 2, 64, 16], F32)
        base2 = R.off
        R2 = Region(self, base2, "spl")
        lp = R2.alloc("lp_sl", [128, 3, 64], F32)
        bsl = R2.alloc("b_sl", [128, 2, 64, 16], F32)
        iota = R2.alloc("iota", [128, 48], F32)
        self.dma("sp", lp, self.lp_sl_d, "s_lp", writes=["sl_in"])
        self.dma("sp", csl, self.c_sl_d, "s_c", writes=["csl"])
        self.dma("sp", bsl, self.b_sl_d, "s_b", writes=["bsl"])
        self.dma("sp", iota, self.iota_d, "s_io", writes=["iota"])
        self.dma("sp", maskp, self.maskp_d, "s_mk", writes=["maskp"])
        ce = self.cexp(R2, 64, lp, "sl_")
        self.op("dve", lambda e: e.memset(pw[:, 0, 0, :], 1.0), pwrites=["pw"])
        self.op("dve", lambda e: e.memset(pw[:, 0, 1:3, :], 0.0), pwrites=["pw"])
        for e_ in range(8):
            self.cmul(pw[:, e_ + 1, 0, :], pw[:, e_ + 1, 1, :], pw[:, e_, 0, :], pw[:, e_, 1, :], ce["lbr"], ce["lbi"],
                      ce["u1"], ce["u2"], ["pw"] + ce["keys"], "pw", ("sl_u1", "sl_u2"), nim=pw[:, e_ + 1, 2, :])
        self.actf(r8, ce["t"], AF.Exp, reads=ce["keys"], writes=["r8"], scale=8.0)
        ys8 = R2.alloc("ys8", [128, 64], F32)
        y8 = R2.alloc("y8", [128, 64], F32)
        self.ts("dve", y8, ce["y"], 8.0, ALU.mult, reads=ce["keys"], writes=["y8"])
        self.frac(ys8, y8, ce["r2"], 0.0, "y8", "ys8", "sl_r2")
        va = R2.alloc("va", [128, 64, 32], F32)
        vt = R2.alloc("vt", [128, 64, 32], F32)
        vr = R2.alloc("vr", [128, 64, 32], F32)
        for (tab, nn, i0, tk) in ((EA, 32, 0, "EA"), (EB, 16, 32, "EB")):
            v_, t_, r_ = va[:, :, 0:nn], vt[:, :, 0:nn], vr[:, :, 0:nn]
            self.tt("dve", v_, ys8.unsqueeze(2).to_broadcast([128, 64, nn]),
                    iota[:, i0:i0 + nn].unsqueeze(1).to_broadcast([128, 64, nn]), ALU.mult, reads=["ys8", "iota"], writes=["va"])
            self.frac(r_, v_, t_, 0.0, "va", "vr", "vt")
            self.actf(tab[:, 1], r_, AF.Sin, reads=["vr"], pwrites=[tk], scale=TWO_PI)
            self.frac(r_, v_, t_, 0.25, "va", "vr", "vt")
            self.actf(tab[:, 0], r_, AF.Sin, reads=["vr"], pwrites=[tk], scale=TWO_PI)
        fbt1 = R2.alloc("fbt1", [128, 64, 16], F32)
        fbt2 = R2.alloc("fbt2", [128, 64, 16], F32)
        self.cmul(FB[:, 0], FB[:, 1], ce["fr"].unsqueeze(2).to_broadcast([128, 64, 16]),
                  ce["fi"].unsqueeze(2).to_broadcast([128, 64, 16]), bsl[:, 0], bsl[:, 1], fbt1, fbt2,
                  ce["keys"] + ["bsl"], "FB", ("fbt1", "fbt2"))
        lpp = R2.alloc("lp_pl", [128, 3, 1024], F32)
        bpl = R2.alloc("b_pl", [128, 2, 1024], F32)
        self.dma("sp", lpp, self.lp_pl_d, "s_lpp", writes=["pl_in"])
        self.dma("sp", bpl, self.b_pl_d, "s_bpl", writes=["bpl"])
        cp = self.cexp(R2, 1024, lpp, "pl_")
        W = [R2.alloc("W%d" % i, [128, 2, 1024], F32) for i in range(2)]
        pwo = [R2.alloc("pwo%d" % i, [128, 2, 1024], BF16) for i in range(2)]
        self.op("dve", lambda e: e.tensor_copy(out=W[0][:, 0, :], in_=cp["fr"]), reads=cp["keys"], pwrites=["W0"])
        self.op("dve", lambda e: e.tensor_copy(out=W[0][:, 1, :], in_=cp["fi"]), reads=cp["keys"], pwrites=["W0"])
        for e_ in range(8):
            wc, wn = W[e_ % 2], W[(e_ + 1) % 2]
            kc, kn = "W%d" % (e_ % 2), "W%d" % ((e_ + 1) % 2)
            po, pk = pwo[e_ % 2], "pwo%d" % (e_ % 2)
            self.cmul(po[:, 0, :], po[:, 1, :], wc[:, 0, :], wc[:, 1, :], bpl[:, 0, :], bpl[:, 1, :], cp["u1"], cp["u2"],
                      [kc, "bpl"], pk, ("pl_u1", "pl_u2"))
            self.dma("sp", self.pwS[:, e_], po, pk, reads=[pk], pwrites=["pwS"])
            if e_ < 7:
                self.cmul(wn[:, 0, :], wn[:, 1, :], wc[:, 0, :], wc[:, 1, :], cp["lbr"], cp["lbi"], cp["u1"], cp["u2"],
                          [kc] + cp["keys"], kn, ("pl_u1", "pl_u2"))
        self.sc.barrier()
        R3 = Region(self, base2, "so")
        U = R3.alloc("U", [128, S], BF16)
        U8 = R3.alloc("U8", [128, 8, N], BF16)
        PWo = R3.alloc("PWo", [128, 8, 2, 128], BF16)
        PWt = [R3.alloc("PWt%d" % i, [128, 8, 2, 2, 128], BF16) for i in range(2)]
        QWt = R3.alloc("QWt", [128, 9 * 2 * 2 * 4 * 128], BF16)
        BWt = R3.alloc("BWt", [128, 2 * 2 * 4 * 128], BF16)
        KT = R3.alloc("KT", [128, 16, 128], BF16)
        Yo = U
        E2 = [R3.alloc("E%d" % i, [128, 2, 2, N], F32) for i in range(2)]
        T1 = R3.alloc("T1", [128, 2, N], F32)
        T2 = R3.alloc("T2", [128, 2, N], F32)
        G1 = R3.alloc("G1", [128, 2, N], F32)
        G2 = R3.alloc("G2", [128, 2, N], F32)
        P1 = R3.alloc("P1", [128, 2, N], F32)
        P2 = R3.alloc("P2", [128, 2, N], F32)
        Wb = R3.alloc("Wb", [128, 2, 2, N], F32)
        Vb = R3.alloc("Vb", [128, 2, 2, N], F32)
        Xb = R3.alloc("Xb", [128, 8, 2, N + 2], BF16)
        ct = [R3.alloc("ct%d" % i, [128, 3, 8, 16], F32) for i in range(4)]
        self.op("pool", lambda e: e.memset(QWt, 0.0), writes=["QWt"])
        self.op("pool", lambda e: e.memset(BWt, 0.0), writes=["BWt"])
        self.op("pool", lambda e: e.memset(Xb, 0.0), writes=["Xb"])
        QW5 = QWt.rearrange("p (e d r q c) -> p e d r q c", e=9, d=2, r=2, q=4)
        BW4 = BWt.rearrange("p (d r q c) -> p d r q c", d=2, r=2, q=4)
        unit = 0
        for o in range(8):
            self.dma("sp", U, self.uS[o], "s_U", reads=["uS"], writes=["U"])
            self.dma("sp", PWo, self.pwS[:, :, :, o * 128:(o + 1) * 128], "s_PWo", reads=["pwS"], writes=["PWo"])
            self.op("act", lambda e: e.activation(out=U8, in_=U.rearrange("p (n i) -> p i n", i=8), func=AF.Copy),
                    reads=["U"], writes=["U8"])
            pd0 = o * 8
            Cre, Cim = csl[:, 0, pd0:pd0 + 8, :], csl[:, 1, pd0:pd0 + 8, :]

            def qfill(es, pd0=pd0, Cre=Cre, Cim=Cim):
                ne, e0 = len(es), es[0]
                Cr = Cre.unsqueeze(1).to_broadcast([128, ne, 8, 16])
                Ci = Cim.unsqueeze(1).to_broadcast([128, ne, 8, 16])
                Pre = pw[:, e0:e0 + ne, 0, pd0:pd0 + 8].unsqueeze(3).to_broadcast([128, ne, 8, 16])
                Pim = pw[:, e0:e0 + ne, 1, pd0:pd0 + 8].unsqueeze(3).to_broadcast([128, ne, 8, 16])
                Pni = pw[:, e0:e0 + ne, 2, pd0:pd0 + 8].unsqueeze(3).to_broadcast([128, ne, 8, 16])
                c0, c1, c2, c3 = [ct[k][:, 0:ne] for k in range(4)]
                self.tt("pool", c0, Cr, Pre, ALU.mult, reads=["csl", "pw"], writes=["ct0"])
                self.tt("pool", c1, Ci, Pim, ALU.mult, reads=["csl", "pw"], writes=["ct1"])
                self.tt("pool", c2, Cr, Pni, ALU.mult, reads=["csl", "pw"], writes=["ct2"])
                self.tt("pool", c3, Ci, Pre, ALU.mult, reads=["csl", "pw"], writes=["ct3"])
                for g2 in range(2):
                    for reim in range(2):
                        dst = _cap(QWt, 64 * g2, 64, e0 * 2048 + reim * 512 + 16 * g2, [[1024, 2 * ne], [160, 4], [1, 16]])
                        a_, b_ = (ct[0], ct[1]) if reim == 0 else (ct[2], ct[3])
                        rowc = a_.ap[0][0]
                        av = bass.AP(tensor=a_.tensor, offset=a_.offset + 64 * g2 * rowc, ap=[[rowc, 64], [64, 2 * ne], [16, 4], [1, 16]])
                        bv = bass.AP(tensor=b_.tensor, offset=b_.offset + 64 * g2 * rowc, ap=[[rowc, 64], [64, 2 * ne], [16, 4], [1, 16]])
                        self.tt("pool", dst, av, bv, ALU.subtract, reads=["ct0", "ct1", "ct2", "ct3"], pwrites=["QWt"])

            def ktstage(pd0=pd0):
                for g2 in range(2):
                    for reim in range(2):
                        dst = _cap(BWt, 64 * g2, 64, reim * 512 + 16 * g2, [[1024, 2], [160, 4], [1, 16]])
                        srcv = FB[64 * g2:64 * g2 + 64, reim, pd0:pd0 + 8, :].rearrange("p (d q) c -> p d q c", d=2)
                        self.op("pool", lambda e, dst=dst, srcv=srcv: e.tensor_copy(out=dst, in_=srcv),
                                reads=["FB"], pwrites=["BWt"])
                kbanks = [self.bank() for _ in range(4)]
                for lag in range(15):
                    dl = lag - 7
                    dirs = [0] if dl > 0 else ([1] if dl < 0 else [0, 1])
                    pairs = []
                    for d_ in dirs:
                        for q in range(4):
                            for reim in range(2):
                                pairs.append((BW4[:, d_, reim, q, :], QW5[:, abs(dl), d_, reim, q, :]))
                    self.mm(kbanks[lag // 4], pairs, reads=["BWt", "QWt"],
                            out=self.ps[:, kbanks[lag // 4], (lag % 4) * 128:(lag % 4 + 1) * 128])
                    if lag % 4 == 3 or lag == 14:
                        kb = kbanks[lag // 4]
                        nl = lag % 4 + 1
                        self.actf(KT[:, (lag // 4) * 4:(lag // 4) * 4 + nl, :],
                                  self.ps[:, kb, 0:nl * 128].rearrange("p (l c) -> p l c", c=128), AF.Copy,
                                  reads=["ps%d" % kb], pwrites=["KT"])
            qparts = [[0, 1, 2], [3, 4], [5, 6], [7, 8]]
            ui = 0
            for d_ in range(2):
                for h in range(2):
                    pt, ptk = PWt[unit % 2], "PWt%d" % (unit % 2)
                    b0 = (unit % 2) * 4
                    unit += 1
                    for reim in range(2):
                        rowP = PWo.ap[0][0]
                        e0 = 7 if d_ == 0 else 0
                        est = -256 if d_ == 0 else 256
                        src_ = bass.AP(tensor=PWo.tensor, offset=PWo.offset + e0 * 256 + reim * 128 + d_ * 64,
                                       ap=[[rowP, 128], [est, 8], [0, 4], [1, 64]])
                        rowT = pt.ap[0][0]
                        dst_ = bass.AP(tensor=pt.tensor, offset=pt.offset + reim * 256, ap=[[rowT, 128], [512, 8], [64, 4], [1, 64]])
                        rowM = maskp.ap[0][0]
                        msk_ = bass.AP(tensor=maskp.tensor, offset=maskp.offset + 4 * h, ap=[[rowM, 128], [0, 8], [1, 4], [0, 64]])
                        self.tt("pool", dst_, src_, msk_, ALU.mult, reads=["PWo", "maskp"], pwrites=[ptk])
                    for reim in range(2):
                        for ql in range(2):
                            bk = b0 + reim * 2 + ql
                            self.mm(bk, [(pt[:, i, reim, ql, :], U8[:, i, :]) for i in range(8)], reads=[ptk, "U8"],
                                    out=self.ps[:, bk, 0:N])
                    qfill(qparts[ui])
                    ui += 1
                    pdl = d_ * 4 + 2 * h
                    pdg = o * 8 + pdl
                    E, ek = E2[unit % 2], "E%d" % (unit % 2)
                    Ev = E.rearrange("p c l (a b) -> p c l a b", b=16)
                    def eab(tab, c):
                        return tab[:, c, pdg:pdg + 2, 0:NA].unsqueeze(3).to_broadcast([128, 2, NA, 16])
                    def ebb(c):
                        return EB[:, c, pdg:pdg + 2, :].unsqueeze(2).to_broadcast([128, 2, NA, 16])
                    t1v = G1.rearrange("p l (a b) -> p l a b", b=16)
                    t2v = G2.rearrange("p l (a b) -> p l a b", b=16)
                    self.tt("dve", t1v, eab(EA, 0), ebb(0), ALU.mult, reads=["EA", "EB"], writes=["G1"])
                    self.tt("dve", t2v, eab(EA, 1), ebb(1), ALU.mult, reads=["EA", "EB"], writes=["G2"])
                    self.tt("dve", Ev[:, 0], t1v, t2v, ALU.subtract, reads=["G1", "G2"], pwrites=[ek])
                    p1v = P1.rearrange("p l (a b) -> p l a b", b=16)
                    p2v = P2.rearrange("p l (a b) -> p l a b", b=16)
                    self.tt("pool", p1v, eab(EA, 1), ebb(0), ALU.mult, reads=["EA", "EB"], writes=["P1"])
                    self.tt("pool", p2v, eab(EA, 0), ebb(1), ALU.mult, reads=["EA", "EB"], writes=["P2"])
                    self.tt("pool", Ev[:, 1], p1v, p2v, ALU.add, reads=["P1", "P2"], pwrites=[ek])
                    Ec = E[:, 0] if d_ == 0 else E[:, 0, :, ::-1]
                    Es = E[:, 1] if d_ == 0 else E[:, 1, :, ::-1]
                    zr = self.ps[:, b0:b0 + 2, 0:N]
                    zi = self.ps[:, b0 + 2:b0 + 4, 0:N]
                    kzr = ["ps%d" % b0, "ps%d" % (b0 + 1)]
                    kzi = ["ps%d" % (b0 + 2), "ps%d" % (b0 + 3)]
                    self.tt("dve", T1, zr, Ec, ALU.mult, reads=kzr + [ek], writes=["T1"])
                    self.tt("dve", T2, zi, Es, ALU.mult, reads=kzi + [ek], writes=["T2"])
                    self.tt("dve", Wb[:, :, 0, :], T1, T2, ALU.add, reads=["T1", "T2"], pwrites=["Wb"])
                    self.tt("dve", T1, zi, Ec, ALU.mult, reads=kzi + [ek], writes=["T1"])
                    self.tt("dve", T2, zr, Es, ALU.mult, reads=kzr + [ek], writes=["T2"])
                    self.tt("dve", Wb[:, :, 1, :], T1, T2, ALU.subtract, reads=["T1", "T2"], pwrites=["Wb"])
                    for ql in range(2):
                        for reim in range(2):
                            wv, vv = Wb[:, ql, reim, :], Vb[:, ql, reim, :]
                            if d_ == 1:
                                wv, vv = wv[:, ::-1], vv[:, ::-1]
                            dec = r8[:, pdg + ql:pdg + ql + 1].to_broadcast([128, N])
                            self.op("dve", lambda e, wv=wv, vv=vv, dec=dec: e.tensor_tensor_scan(
                                out=vv, data0=dec, data1=wv, initial=0.0, op0=ALU.mult, op1=ALU.add),
                                reads=["Wb", "r8"], pwrites=["Vb"])
                    xr = Xb[:, pdl:pdl + 2, 0, 1:N + 1]
                    xi = Xb[:, pdl:pdl + 2, 1, 1:N + 1]
                    self.tt("dve", T1, Vb[:, :, 0, :], Ec, ALU.mult, reads=["Vb", ek], writes=["T1"])
                    self.tt("dve", T2, Vb[:, :, 1, :], Es, ALU.mult, reads=["Vb", ek], writes=["T2"])
                    self.tt("dve", xr, T1, T2, ALU.subtract, reads=["T1", "T2"], pwrites=["Xb"])
                    self.tt("dve", G1, Vb[:, :, 0, :], Es, ALU.mult, reads=["Vb", ek], writes=["G1"])
                    self.tt("dve", G2, Vb[:, :, 1, :], Ec, ALU.mult, reads=["Vb", ek], writes=["G2"])
                    self.tt("dve", xi, G1, G2, ALU.add, reads=["G1", "G2"], pwrites=["Xb"])
            ktstage()
            Y8 = Yo.rearrange("p (n i) -> p i n", i=8)
            for j in range(8):
                pairs = [(KT[:, (j - i) + 7, :], U8[:, i, :]) for i in range(8)]
                for d_ in range(2):
                    e_ = (j + 1) if d_ == 0 else (8 - j)
                    off = 0 if d_ == 0 else 2
                    for q in range(4):
                        for reim in range(2):
                            pairs.append((QW5[:, e_, d_, reim, q, :], Xb[:, d_ * 4 + q, reim, off:off + N]))
                bk = self.bank()
                self.mm(bk, pairs, reads=["KT", "U8", "QWt", "Xb"], out=self.ps[:, bk, 0:N])
                self.stt(Y8[:, j, :], U8[:, j, :], self.small[:, 18 + o:19 + o], self.ps[:, bk, 0:N], ALU.mult, ALU.add,
                         reads=["U8", "c_small", "ps%d" % bk], pwrites=["U"])
            self.dma("act", self.ysS[o], Yo, "s_Y", reads=["U"], pwrites=["ysS"])
            if self.dbg_barrier:
                self.sc.barrier()


def _bf16(a):
    import ml_dtypes
    return np.asarray(a).astype(ml_dtypes.bfloat16)


def host_consts(S):
    c = {}
    c["ident"] = _bf16(np.eye(128, dtype=np.float32))
    c["onesm"] = _bf16(np.ones((128, 128), np.float32))
    rot = np.zeros((128, 128), np.float32)
    for m in range(64):
        rot[m + 64, m] = -1.0
        rot[m, m + 64] = 1.0
    c["rotm"] = rot
    pos = np.arange(S, dtype=np.float32)
    inv = (10000.0 ** (-np.arange(0, 128, 2, dtype=np.float32) / 128)).astype(np.float32)
    ang = pos[None, :] * inv[:, None]
    c["cosT"] = np.concatenate([np.cos(ang), np.cos(ang)], 0).astype(np.float32)
    c["sinT"] = np.concatenate([np.sin(ang), np.sin(ang)], 0).astype(np.float32)
    kl = np.arange(128)[:, None]
    ql = np.arange(128)[None, :]
    c["trimask"] = _bf16(np.stack([(kl >= ql), (kl <= ql)], 1).astype(np.float32))
    c["iotaab"] = np.tile(np.concatenate([16.0 * np.arange(32), np.arange(16)]).astype(np.float32)[None, :], (128, 1))
    mp = np.zeros((128, 4, 2), np.float32)
    for r in range(128):
        gl = r // 16
        mp[r, gl // 2, gl % 2] = 1.0
    c["maskp"] = mp
    return c


def host_params(p):
    o = {}
    o["w_f1gu"] = p["ffn1_w_gate_up"]; o["w_f1dn"] = p["ffn1_w_down"]; o["w_win"] = p["w_in"]
    o["w_wglu"] = p["w_glu"]; o["w_wao"] = p["w_attn_out"]; o["w_wso"] = p["w_ssm_out"]; o["w_wo"] = p["w_out"]
    o["w_f2gu"] = p["ffn2_w_gate_up"]; o["w_f2dn"] = p["ffn2_w_down"]
    g3 = np.stack([p["ffn1_norm"], p["mix_norm"], p["ffn2_norm"]], 0)
    o["gT3"] = np.ascontiguousarray(g3.reshape(3, 16, 128).transpose(2, 0, 1))
    sm = np.zeros((128, 32), np.float32)
    sm[:, 0] = p["q_norm"]; sm[:, 1] = p["k_norm"]
    sm[:, 2:10] = p["attn_sink"][None, :]
    sm[:, 10:18] = p["b_glu"].reshape(8, 128).T
    sm[:, 18:26] = p["ssm_d"].reshape(8, 128).T
    o["small"] = sm
    lre, lim, ldt = p["ssm_lambda_re"], p["ssm_lambda_im"], p["ssm_log_dt"]
    bre, bim, cre, cim = p["ssm_b_re"], p["ssm_b_im"], p["ssm_c_re"], p["ssm_c_im"]
    def pl(a3):
        t = a3.reshape(2, 8, 8, 64)
        t = t.transpose(2, 1, 0, 3)
        t = np.repeat(t[:, None], 16, axis=1)
        return t.reshape(128, 1024)
    ldt3 = np.repeat(ldt[:, :, None], 64, axis=2)
    o["lp_pl"] = np.ascontiguousarray(np.stack([pl(lre), pl(lim), pl(ldt3)], 1)).astype(np.float32)
    def plb(b4):
        t = b4.reshape(2, 8, 8, 64, 16).transpose(2, 4, 1, 0, 3)
        return t.reshape(128, 1024)
    o["b_pl"] = np.ascontiguousarray(np.stack([plb(bre), plb(bim)], 1)).astype(np.float32)
    def sl(a3):
        t = a3.reshape(2, 8, 4, 2, 64)
        t = t.transpose(3, 4, 1, 0, 2)
        return t.reshape(128, 64)
    o["lp_sl"] = np.ascontiguousarray(np.stack([sl(lre), sl(lim), sl(ldt3)], 1)).astype(np.float32)
    def slc(c4):
        t = c4.reshape(2, 8, 4, 2, 16, 64).transpose(3, 5, 1, 0, 2, 4)
        return t.reshape(128, 64, 16)
    def slb(b4):
        t = b4.reshape(2, 8, 4, 2, 64, 16).transpose(3, 4, 1, 0, 2, 5)
        return t.reshape(128, 64, 16)
    o["c_sl"] = np.ascontiguousarray(np.stack([slc(cre), slc(cim)], 1)).astype(np.float32)
    o["b_sl"] = np.ascontiguousarray(np.stack([slb(bre), slb(bim)], 1)).astype(np.float32)
    return o


def core_inputs(xc, nvalid, S, consts, params):
    m = dict(consts)
    m.update(params)
    m["x"] = np.ascontiguousarray(xc, dtype=np.float32)
    v = (np.arange(S) < nvalid).astype(np.float32)
    m["validcol"] = np.ascontiguousarray(v.reshape(S // 128, 128).T)
    m["validrep"] = _bf16(np.repeat(v[:, None], 128, axis=1))
    return m


def kernel(**inputs):
    S = 4096
    x_prompt = np.asarray(inputs["x_prompt"], dtype=np.float32)
    x_sample = np.asarray(inputs["x_sample"], dtype=np.float32)
    p = {k: np.asarray(v)[0] for k, v in inputs.items() if k not in ("x_prompt", "x_sample")}
    consts = host_consts(S)
    params = host_params(p)
    in_maps = []
    for i in range(4):
        in_maps.append(core_inputs(x_prompt[i], S, S, consts, params))
    ns = x_sample.shape[1]
    for i in range(4):
        xp = np.zeros((S, D), np.float32)
        xp[:ns] = x_sample[i]
        in_maps.append(core_inputs(xp, ns, S, consts, params))
    prog = Prog(S)
    res = run_bass_kernel_spmd(prog.nc, in_maps, core_ids=list(range(8)))
    y_prompt = np.stack([np.asarray(res.results[i]["y"], dtype=np.float32) for i in range(4)], 0)
    y_sample = np.stack([np.asarray(res.results[4 + i]["y"], dtype=np.float32)[:ns] for i in range(4)], 0)
    return (y_prompt, y_sample)
```

```python
import numpy as np
import concourse.bass as bass
import concourse.mybir as mybir
from concourse.bass_utils import run_bass_kernel_spmd

F32 = mybir.dt.float32
BF16 = mybir.dt.bfloat16
AF = mybir.ActivationFunctionType
ALU = mybir.AluOpType
AX = mybir.AxisListType

D = 2048
DFF = 5632
NFF = DFF // 128
INW = 6656
EPS = 1e-6


class _Op:
    __slots__ = ("eng", "fn", "waits", "signal", "count", "dma")

    def __init__(self, eng, fn, waits, dma):
        self.eng = eng
        self.fn = fn
        self.waits = waits
        self.signal = False
        self.count = 0
        self.dma = dma


class Sched:
    ENGS = ("pe", "act", "dve", "pool", "sp")

    def __init__(self, nc):
        self.nc = nc
        self.ops = {e: [] for e in self.ENGS}
        self.state = {}
        self.dma_count = {}

    @staticmethod
    def _src(tok):
        return ("e", tok[1]) if tok[0] == "eng" else ("d", tok[1])

    @staticmethod
    def _newer(a, b):
        return a[2] > b[2]

    def op(self, eng, fn, reads=(), writes=(), pwrites=(), dma=None):
        deps = {}

        def add(tok):
            s = self._src(tok)
            if s not in deps or self._newer(tok, deps[s]):
                deps[s] = tok

        for b in reads:
            st = self.state.get(b)
            if st:
                for t in st[0].values():
                    add(t)
        for b in list(writes) + list(pwrites):
            st = self.state.get(b)
            if st:
                for t in st[0].values():
                    add(t)
                for t in st[1].values():
                    add(t)
        o = _Op(eng, fn, list(deps.values()), dma)
        idx = len(self.ops[eng])
        self.ops[eng].append(o)
        if dma is not None:
            n = self.dma_count.get(dma, 0) + 1
            self.dma_count[dma] = n
            tok = ("dma", dma, 16 * n)
        else:
            tok = ("eng", eng, idx)
        s = self._src(tok)
        for b in reads:
            st = self.state.setdefault(b, [{}, {}])
            st[1][s] = tok
        for b in writes:
            self.state[b] = [{s: tok}, {}]
        for b in pwrites:
            st = self.state.setdefault(b, [{}, {}])
            st[0][s] = tok
        return tok

    def barrier(self):
        toks = []
        for e in self.ENGS:
            for i in range(len(self.ops[e]) - 1, -1, -1):
                if self.ops[e][i].dma is None:
                    toks.append(("eng", e, i))
                    break
        for k, n in self.dma_count.items():
            toks.append(("dma", k, 16 * n))
        for e in self.ENGS:
            self.ops[e].append(_Op(e, lambda eng: eng.nop(), list(toks), None))
        self.state = {}

    def emit(self, final_wait_keys=()):
        nc = self.nc
        for e in self.ENGS:
            for o in self.ops[e]:
                for t in o.waits:
                    if t[0] == "eng":
                        self.ops[t[1]][t[2]].signal = True
        for e in self.ENGS:
            c = 0
            for o in self.ops[e]:
                if o.signal:
                    c += 1
                o.count = c
        engobj = {"pe": nc.tensor, "act": nc.scalar, "dve": nc.vector, "pool": nc.gpsimd, "sp": nc.sync}
        import contextlib
        with contextlib.ExitStack() as es:
            esem = {e: es.enter_context(nc.semaphore("s_" + e)) for e in self.ENGS}
            dsem = {k: es.enter_context(nc.semaphore("d_%d" % i)) for i, k in enumerate(self.dma_count)}
            block = es.enter_context(nc.Block())
            ops = self.ops
            dma_count = self.dma_count

            def run(e, eng):
                known = {}
                for o in ops[e]:
                    need = {}
                    for t in o.waits:
                        if t[0] == "eng":
                            if t[1] == e and e == "pe":
                                continue
                            src = ops[t[1]][t[2]]
                            key = ("e", t[1])
                            val = src.count
                        else:
                            key = ("d", t[1])
                            val = t[2]
                        if val > need.get(key, 0):
                            need[key] = val
                    for key, val in need.items():
                        if known.get(key, 0) >= val:
                            continue
                        known[key] = val
                        sem = esem[key[1]] if key[0] == "e" else dsem[key[1]]
                        eng.wait_ge(sem, val)
                    ins = o.fn(eng)
                    if o.dma is not None:
                        ins.then_inc(dsem[o.dma], 16)
                    elif o.signal:
                        ins.then_inc(esem[e], 1)
                if e == "sp":
                    for k in dma_count:
                        eng.wait_ge(dsem[k], 16 * dma_count[k])

            @block.tensor
            def _(eng):
                run("pe", eng)

            @block.scalar
            def _(eng):
                run("act", eng)

            @block.vector
            def _(eng):
                run("dve", eng)

            @block.gpsimd
            def _(eng):
                run("pool", eng)

            @block.sync
            def _(eng):
                run("sp", eng)


MAGIC = 12582912.0
TWO_PI = float(2.0 * np.pi)
WSLOT = 12 * 1024
NWS = 4


def _cap(base, part0, nparts, col_off, dims):
    row = base.ap[0][0]
    return bass.AP(tensor=base.tensor, offset=part0 * row + col_off, ap=[[row, nparts]] + [list(d) for d in dims])


class Region:
    def __init__(self, prog, base, tag):
        self.p, self.off, self.tag, self.base = prog, base, tag, base

    def alloc(self, name, shape, dtype):
        nbytes = int(np.prod(shape[1:])) * mybir.dt.size(dtype)
        off = (self.off + 31) // 32 * 32
        self.off = off + nbytes
        assert self.off <= 229376, (self.tag, name, self.off)
        self.p.uid += 1
        h = self.p.nc.alloc_sbuf_tensor_at("%s_%s_%d" % (self.tag, name, self.p.uid), list(shape), dtype, offset=off)
        return h.ap()


class Prog:
    def __init__(self, S, dbg=(), phases="PASB", extin=()):
        self.extin = set(extin)
        import os
        self.dbg_barrier = bool(int(os.environ.get("DBG_BARRIER", "0")))
        self.S = S
        self.NT = S // 512
        self.NB = S // 128
        self.dbg = set(dbg)
        self.phases = phases
        self.nc = bass.Bass("TRN2", target_bir_lowering=False)
        self.sc = Sched(self.nc)
        self.uid = 0
        self.outkeys = []
        self.bankrr = 0
        self.build()

    def din(self, name, shape, dtype=F32):
        return self.nc.dram_tensor(name, list(shape), dtype, kind="ExternalInput").ap()

    def dscr(self, name, shape, dtype):
        kind = "ExternalOutput" if name in self.dbg else "Internal"
        if name in self.extin:
            kind = "ExternalInput"
        if name in self.dbg:
            self.outkeys.append(("w" + name) if name.startswith("s_") else name)
        return self.nc.dram_tensor(name, list(shape), dtype, kind=kind).ap()

    def op(self, eng, fn, reads=(), writes=(), pwrites=()):
        return self.sc.op(eng, fn, reads=reads, writes=writes, pwrites=pwrites)

    def dma(self, eng, out, in_, key, reads=(), writes=(), pwrites=(), nc_ok=False):
        if nc_ok:
            fn = lambda e: e.dma_start(out=out, in_=in_, allow_slow_non_contiguous=True)
        else:
            fn = lambda e: e.dma_start(out=out, in_=in_)
        return self.sc.op(eng, fn, reads=reads, writes=writes, pwrites=pwrites, dma=key)

    def bank(self):
        b = self.bankrr
        self.bankrr = (b + 1) % 8
        return b

    def wslot_view(self, slot, shape, dtype=BF16):
        n = int(np.prod(shape[1:]))
        assert n * mybir.dt.size(dtype) <= WSLOT
        base = self.wring[slot]
        flat = base[:, 0:n] if dtype == BF16 else base.bitcast(F32)[:, 0:n]
        names = "abcdefg"[: len(shape) - 1]
        if len(shape) == 2:
            return flat
        pat = "p (" + " ".join(names) + ") -> p " + " ".join(names)
        kw = {names[i]: shape[i + 1] for i in range(len(names))}
        return flat.rearrange(pat, **kw)

    def wstream(self, descs):
        st = {"issued": 0}
        prog = self

        def issue(i):
            shape, parts = descs[i]
            slot = prog.wcount % NWS
            prog.wcount += 1
            v = prog.wslot_view(slot, shape)
            for sel, src, skey in parts:
                prog.dma("sp", sel(v), src, "wr%d" % slot, reads=[skey], pwrites=["wr%d" % slot])
            return v, "wr%d" % slot

        got = {}

        def get(i):
            while st["issued"] < min(len(descs), i + NWS):
                got[st["issued"]] = issue(st["issued"])
                st["issued"] += 1
            return got.pop(i)

        return get

    def build(self):
        S, NT = self.S, self.NT
        self.x = self.din("x", [S, D])
        self.wdefs = {
            "f1gu": (D, 2 * DFF), "f1dn": (DFF, D), "win": (D, INW), "wglu": (1024, 1024),
            "wao": (1024, D), "wso": (1024, D), "wo": (D, D), "f2gu": (D, 2 * DFF), "f2dn": (DFF, D)}
        self.wsrc = {n: self.din("w_" + n, [k, c]) for n, (k, c) in self.wdefs.items()}
        self.gT3_d = self.din("gT3", [128, 3, 16])
        self.small_d = self.din("small", [128, 32])
        self.ident_d = self.din("ident", [128, 128], BF16)
        self.ones_d = self.din("onesm", [128, 128], BF16)
        self.rotm_d = self.din("rotm", [128, 128])
        self.cos_d = self.din("cosT", [128, S])
        self.sin_d = self.din("sinT", [128, S])
        self.tri_d = self.din("trimask", [128, 2, 128], BF16)
        self.valid_d = self.din("validcol", [128, self.NB])
        self.vrep_d = self.din("validrep", [S, 128], BF16)
        self.lp_pl_d = self.din("lp_pl", [128, 3, 1024])
        self.b_pl_d = self.din("b_pl", [128, 2, 1024])
        self.lp_sl_d = self.din("lp_sl", [128, 3, 64])
        self.c_sl_d = self.din("c_sl", [128, 2, 64, 16])
        self.b_sl_d = self.din("b_sl", [128, 2, 64, 16])
        self.iota_d = self.din("iotaab", [128, 48])
        self.maskp_d = self.din("maskp", [128, 4, 2])
        self.y = self.nc.dram_tensor("y", [S, D], F32, kind="ExternalOutput").ap()
        self.wscr = {n: self.dscr("s_" + n, [128, c // 128, k // 128, 128], BF16) for n, (k, c) in self.wdefs.items()}
        self.x1 = self.dscr("x1", [S, D], F32)
        self.uS = self.dscr("uS", [8, 128, S], BF16)
        self.kS = self.dscr("kS", [2, 128, S], BF16)
        self.vS = self.dscr("vS", [S, 256], BF16)
        self.ysS = self.dscr("ysS", [8, 128, S], BF16)
        self.pwS = self.dscr("pwS", [128, 8, 2, 1024], BF16)
        self.ps = self.nc.alloc_psum_tensor("ps", [128, 8, 512], F32).ap()
        R = Region(self, 16640, "c")
        self.ident = R.alloc("ident", [128, 128], BF16)
        self.onesm = R.alloc("onesm", [128, 128], BF16)
        self.rotm = R.alloc("rotm", [128, 128], F32)
        self.gT3 = R.alloc("gT3", [128, 3, 16], F32)
        self.small = R.alloc("small", [128, 32], F32)
        self.esink = R.alloc("esink", [128, 8], F32)
        self.tri = R.alloc("tri", [128, 2, 128], BF16)
        self.valid = R.alloc("valid", [128, self.NB], F32)
        self.epsc = R.alloc("epsc", [128, 8], F32)
        self.op("dve", lambda e: e.memset(self.epsc, EPS), writes=["epsc"])
        self.cbase = (R.off + 63) // 64 * 64
        for nm, dst, src in (("ident", self.ident, self.ident_d), ("onesm", self.onesm, self.ones_d),
                             ("rotm", self.rotm, self.rotm_d), ("gT3", self.gT3, self.gT3_d),
                             ("small", self.small, self.small_d), ("tri", self.tri, self.tri_d),
                             ("valid", self.valid, self.valid_d)):
            self.dma("sp", dst, src, "c_" + nm, writes=["c_" + nm])
        self.op("act", lambda e: e.activation(out=self.esink, in_=self.small[:, 2:10], func=AF.Exp),
                reads=["c_small"], writes=["esink"])
        self.wcount = 0
        if "P" in self.phases:
            self.precast()
        if "A" in self.phases:
            self.phaseAB("A")
        if "S" in self.phases:
            self.sc.barrier()
            self.phaseS()
        if "B" in self.phases:
            self.sc.barrier()
            self.phaseAB("B")
        self.sc.emit(final_wait_keys=self.outkeys)

    def wchunks(self, n):
        K, C = self.wdefs[n]
        nk = K // 128
        if n in ("f1gu", "f2gu"):
            cs = [(0, 2816), (5632, 8448), (2816, 5632), (8448, 11264)]
            return [(c0, c1, 0, nk) for c0, c1 in cs]
        if n in ("f1dn", "f2dn"):
            return [(0, C, 0, 22), (0, C, 22, 44)]
        if n == "win":
            return [(0, 3328, 0, nk), (3328, 6656, 0, nk)]
        return [(0, C, 0, nk)]

    def wk(self, n, ct, kc=0):
        c = ct * 128
        for i, (c0, c1, k0, k1) in enumerate(self.wchunks(n)):
            if c0 <= c < c1 and k0 <= kc < k1:
                return "ws_%s_%d" % (n, i)
        raise KeyError((n, ct, kc))

    def precast(self):
        def cast(n, idxs=None):
            src, dst = self.wsrc[n], self.wscr[n]
            for i, (c0, c1, k0, k1) in enumerate(self.wchunks(n)):
                if idxs is not None and i not in idxs:
                    continue
                key = "ws_%s_%d" % (n, i)
                ncol = c1 - c0
                g = max(1, min(k1 - k0, 4096 // ncol))
                for kc in range(k0, k1, g):
                    gg = min(g, k1 - kc)
                    s = src[kc * 128:(kc + gg) * 128, c0:c1].rearrange("(g p) (ct j) -> p ct g j", p=128, j=128)
                    d = dst[:, c0 // 128:c1 // 128, kc:kc + gg, :]
                    self.dma("pool", d, s, key, pwrites=[key])
        cast("f1gu", [0, 1])
        cast("f1dn", [0])
        cast("f1gu", [2, 3])
        cast("f1dn", [1])
        for n in ["win", "wglu", "wao", "wso", "wo"]:
            cast(n)
        cast("f2gu", [0, 1])
        cast("f2dn", [0])
        cast("f2gu", [2, 3])
        cast("f2dn", [1])

    def mm(self, bank, pairs, reads, start=True, stop=True, out=None):
        out_ap = self.ps[:, bank, :] if out is None else out

        def fn(e, pairs=pairs, out_ap=out_ap, start=start, stop=stop):
            n = len(pairs)
            for i, (l, r) in enumerate(pairs):
                ins = e.matmul(out_ap, lhsT=l, rhs=r, start=(start and i == 0), stop=(stop and i == n - 1))
            return ins
        return self.op("pe", fn, reads=reads, writes=["ps%d" % bank])

    def tt(self, eng, out, in0, in1, op, reads, writes=(), pwrites=()):
        return self.op(eng, lambda e: e.tensor_tensor(out=out, in0=in0, in1=in1, op=op), reads=reads, writes=writes, pwrites=pwrites)

    def ts(self, eng, out, in0, s1, op0, reads, writes=(), pwrites=(), s2=None, op1=None):
        if op1 is None:
            fn = lambda e: e.tensor_scalar(out=out, in0=in0, scalar1=s1, scalar2=None, op0=op0)
        else:
            fn = lambda e: e.tensor_scalar(out=out, in0=in0, scalar1=s1, scalar2=s2, op0=op0, op1=op1)
        return self.op(eng, fn, reads=reads, writes=writes, pwrites=pwrites)

    def stt(self, out, in0, scalar, in1, op0, op1, reads, writes=(), pwrites=()):
        return self.op("dve", lambda e: e.scalar_tensor_tensor(out=out, in0=in0, scalar=scalar, in1=in1, op0=op0, op1=op1),
                       reads=reads, writes=writes, pwrites=pwrites)

    def actf(self, out, in_, func, reads, writes=(), pwrites=(), scale=1.0, bias=None):
        if bias is None:
            fn = lambda e: e.activation(out=out, in_=in_, func=func, scale=scale)
        else:
            fn = lambda e: e.activation(out=out, in_=in_, func=func, scale=scale, bias=bias)
        return self.op("act", fn, reads=reads, writes=writes, pwrites=pwrites)

    def alloc_AB(self, ph):
        R = Region(self, self.cbase, "ab" + ph)
        self.xt = R.alloc("xt", [128, 4, D], F32)
        self.xn = R.alloc("xn", [128, 2, D], BF16)
        self.ss4 = R.alloc("ss4", [128, 4], F32)
        self.rs4 = R.alloc("rs4", [128, 4], F32)
        self.Hbuf = [(R.alloc("hT", [128, 16, 512], BF16), "hT0"), (R.alloc("hT2", [128, 16, 512], BF16), "hT1")]
        self.ssp = R.alloc("ssp", [128, 4], F32)
        self.rsp = R.alloc("rsp", [128, 4], F32)
        self.use_h(0)
        self.actT = R.alloc("actT", [128, 22, 512], BF16)
        self.mergedT = self.actT[:, 0:16, :]
        self.sg = R.alloc("sg", [128, 2, 512], F32)
        self.wring = [R.alloc("wr%d" % i, [128, WSLOT // 2], BF16) for i in range(NWS)]
        self.cst = R.alloc("cst", [128, 2, 512], F32)
        self.TT = R.alloc("TT", [128, 4096], F32)
        self.TTb = self.TT.bitcast(BF16)
        self.T = [self.TT[:, i * 512:(i + 1) * 512] for i in range(8)]
        if ph == "A":
            self.uT = R.alloc("uT", [128, 8, 512], BF16)
            self.kT = R.alloc("kT", [128, 2, 512], BF16)
            self.vx = R.alloc("vx", [128, 4, 256], BF16)
        else:
            self.qT = R.alloc("qT", [128, 8, 512], BF16)
            self.kwin = R.alloc("kwin", [128, 2, 768], BF16)
            self.vwin = R.alloc("vwin", [128, 6, 256], BF16)
            self.vrw = R.alloc("vrw", [128, 6, 128], BF16)
            self.attnT = R.alloc("attnT", [128, 8, 512], BF16)
            self.zT = R.alloc("zT", [128, 8, 512], BF16)
            self.z2T = R.alloc("z2T", [128, 8, 512], BF16)

    def use_h(self, i):
        self.hT, self.hk = self.Hbuf[i]

    def prenorm_a(self, rows_ap, b):
        tk = ["T0", "T1", "T2", "T3"]
        xs = self.TT[:, 0:2048]
        self.dma("sp", xs, rows_ap, "xs", writes=tk)
        xb, xk = self.xn[:, b % 2, :], "xn%d" % (b % 2)
        self.actf(xb, xs, AF.Square, reads=tk, writes=[xk])
        sspb, rspb = self.ssp[:, b:b + 1], self.rsp[:, b:b + 1]
        self.op("dve", lambda e: e.tensor_reduce(out=sspb, in_=xb, axis=AX.X, op=ALU.add),
                reads=[xk], writes=["ssp%d" % b])
        self.actf(self.rsp[:, b:b + 1], self.ssp[:, b:b + 1], AF.Sqrt, reads=["ssp%d" % b, "epsc"], writes=["rsp%d" % b],
                  scale=1.0 / D, bias=self.epsc[:, 0:1])
        self.op("dve", lambda e: e.reciprocal(out=rspb, in_=rspb),
                reads=["rsp%d" % b], writes=["rsp%d" % b])
        self.actf(xb, xs, AF.Copy, reads=tk + ["rsp%d" % b], writes=[xk], scale=self.rsp[:, b:b + 1])

    def prenorm_b(self, b, gi, hi):
        H, hk = self.Hbuf[hi]
        xb, xk = self.xn[:, b % 2, :], "xn%d" % (b % 2)
        b0 = self.bank()
        if b0 % 2 == 1:
            b0 = self.bank()
        self.bank()
        pst = [self.ps[:, b0, :].bitcast(BF16), self.ps[:, b0 + 1, :].bitcast(BF16)]

        def fn(e, xb=xb, pst=pst):
            for c in range(16):
                ins = e.transpose(out=pst[c // 8][:, (c % 8) * 128:(c % 8 + 1) * 128], in_=xb[:, c * 128:(c + 1) * 128],
                                  identity=self.ident)
            return ins
        self.op("pe", fn, reads=[xk, "c_ident"], writes=["ps%d" % b0, "ps%d" % (b0 + 1)])
        for hb in range(2):
            self.tt("dve", H[:, hb * 8:hb * 8 + 8, b * 128:(b + 1) * 128],
                    pst[hb].rearrange("p (c t) -> p c t", t=128),
                    self.gT3[:, gi, hb * 8:hb * 8 + 8].unsqueeze(2).to_broadcast([128, 8, 128]), ALU.mult,
                    reads=["ps%d" % (b0 + hb), "c_gT3"], pwrites=[hk])

    def norm_hT(self, gi):
        for b in range(4):
            xb = self.xn[:, b % 2, :]
            self.actf(xb, self.xt[:, b, :], AF.Square, reads=["xt%d" % b], writes=["xn%d" % (b % 2)])
            ss4b = self.ss4[:, b:b + 1]
            self.op("dve", lambda e, ss4b=ss4b, xb=xb: e.tensor_reduce(out=ss4b, in_=xb, axis=AX.X, op=ALU.add),
                    reads=["xn%d" % (b % 2)], pwrites=["ss4"])
        self.actf(self.rs4, self.ss4, AF.Sqrt, reads=["ss4", "epsc"], writes=["rs4"], scale=1.0 / D, bias=self.epsc[:, 0:1])
        rs4 = self.rs4
        self.op("dve", lambda e: e.reciprocal(out=rs4, in_=rs4), reads=["rs4"], writes=["rs4"])
        for b in range(4):
            xb = self.xn[:, b % 2, :]
            self.actf(xb, self.xt[:, b, :], AF.Copy, reads=["xt%d" % b, "rs4"], writes=["xn%d" % (b % 2)],
                      scale=self.rs4[:, b:b + 1])
            b0 = self.bank()
            if b0 % 2 == 1:
                b0 = self.bank()
            self.bank()
            pst = [self.ps[:, b0, :].bitcast(BF16), self.ps[:, b0 + 1, :].bitcast(BF16)]

            def fn(e, xb=xb, pst=pst):
                for c in range(16):
                    ins = e.transpose(out=pst[c // 8][:, (c % 8) * 128:(c % 8 + 1) * 128], in_=xb[:, c * 128:(c + 1) * 128],
                                      identity=self.ident)
                return ins
            self.op("pe", fn, reads=["xn%d" % (b % 2), "c_ident"], writes=["ps%d" % b0, "ps%d" % (b0 + 1)])
            for hb in range(2):
                self.tt("dve", self.hT[:, hb * 8:hb * 8 + 8, b * 128:(b + 1) * 128],
                        pst[hb].rearrange("p (c t) -> p c t", t=128),
                        self.gT3[:, gi, hb * 8:hb * 8 + 8].unsqueeze(2).to_broadcast([128, 8, 128]), ALU.mult,
                        reads=["ps%d" % (b0 + hb), "c_gT3"], pwrites=[self.hk])

    def ffn(self, wgu, wdn, store=None, hooks=None):
        hooks = hooks or {}
        descs = []
        for half in range(2):
            for jj in range(22):
                j = half * 22 + jj
                descs.append(([128, 2, 16, 128],
                              [(lambda v: v[:, 0], self.wscr[wgu][:, j], self.wk(wgu, j)),
                               (lambda v: v[:, 1], self.wscr[wgu][:, 44 + j], self.wk(wgu, 44 + j))]))
            for s in range(4):
                for kh in range(2):
                    k0 = half * 22 + kh * 11
                    descs.append(([128, 4, 11, 128],
                                  [(lambda v: v, self.wscr[wdn][:, 4 * s:4 * s + 4, k0:k0 + 11, :], self.wk(wdn, 4 * s, k0))]))
        get = self.wstream(descs)
        di = 0
        for half in range(2):
            for jj in range(22):
                w, wk = get(di)
                di += 1
                bg, bu = self.bank(), self.bank()
                self.mm(bg, [(w[:, 0, k, :], self.hT[:, k, :]) for k in range(16)], reads=[wk, self.hk])
                self.mm(bu, [(w[:, 1, k, :], self.hT[:, k, :]) for k in range(16)], reads=[wk, self.hk])
                si = jj % 2
                self.actf(self.sg[:, si, :], self.ps[:, bg, :], AF.Silu, reads=["ps%d" % bg], writes=["sg%d" % si])
                self.tt("dve", self.actT[:, jj, :], self.sg[:, si, :], self.ps[:, bu, :], ALU.mult,
                        reads=["sg%d" % si, "ps%d" % bu], writes=["act%d" % jj])
                if ("gu", half, jj) in hooks:
                    hooks[("gu", half, jj)]()
            for s in range(4):
                if ("dn", half, s) in hooks:
                    hooks[("dn", half, s)]()
                banks = [self.bank() for _ in range(4)]
                for kh in range(2):
                    w, wk = get(di)
                    di += 1
                    for tb in range(4):
                        pairs = [(self.actT[:, kh * 11 + j2, tb * 128:(tb + 1) * 128], w[:, :, j2, :]) for j2 in range(11)]
                        self.mm(banks[tb], pairs, reads=[wk] + ["act%d" % (kh * 11 + j2) for j2 in range(11)],
                                start=(kh == 0), stop=(kh == 1))
                for tb in range(4):
                    sl = self.xt[:, tb, s * 512:(s + 1) * 512]
                    self.stt(sl, self.ps[:, banks[tb], :], 0.5, sl, ALU.mult, ALU.add,
                             reads=["ps%d" % banks[tb], "xt%d" % tb], pwrites=["xt%d" % tb])
                    if store is not None and half == 1:
                        store(tb, s)

    def qk_head(self, bank_in, gcol, out_ap, okey, si, e2="dve", split=False):
        T = self.T
        sq, rv, qn, t1 = self.TTb[:, 4 * si * 1024:4 * si * 1024 + 512], T[4 * si + 1], T[4 * si + 2], T[4 * si + 3]
        ks, kr, kq, k1 = ["T%d" % (4 * si + i) for i in range(4)]
        pin = self.ps[:, bank_in, :]
        self.actf(sq, pin, AF.Square, reads=["ps%d" % bank_in], writes=[ks])
        b2 = self.bank()
        self.mm(b2, [(self.onesm, sq)], reads=["c_onesm", ks])
        self.actf(rv, self.ps[:, b2, :], AF.Ln, reads=["ps%d" % b2, "epsc"], writes=[kr], scale=1.0 / 128, bias=self.epsc[:, 0:1])
        self.actf(rv, rv, AF.Exp, reads=[kr], writes=[kr], scale=-0.5)
        self.stt(qn, pin, self.small[:, gcol:gcol + 1], rv, ALU.mult, ALU.mult, reads=["ps%d" % bank_in, kr, "c_small"], writes=[kq])
        b3 = self.bank()
        self.mm(b3, [(self.rotm, qn)], reads=["c_rotm", kq])

        def stage_b():
            self.tt(e2, t1, qn, self.cst[:, 0, :], ALU.mult, reads=[kq, "cst"], writes=[k1])
            self.tt("dve", rv, self.ps[:, b3, :], self.cst[:, 1, :], ALU.mult, reads=["ps%d" % b3, "cst"], writes=[kr])
            self.tt(e2, out_ap, t1, rv, ALU.add, reads=[k1, kr], pwrites=[okey])
        if split:
            return stage_b
        stage_b()

    def phaseAB(self, ph):
        S, NT = self.S, self.NT
        self.alloc_AB(ph)
        xsrc = self.x if ph == "A" else self.x1
        xsk = "x" if ph == "A" else "x1"
        g_pre = 0 if ph == "A" else 1

        def preA(tn, b):
            rr = tn * 512 + b * 128
            self.prenorm_a(xsrc[rr:rr + 128, :], b)

        def preB(b):
            self.prenorm_b(b, g_pre, 0)
        for b in range(4):
            preA(0, b)
            preB(b)
        for t in range(NT):
            r0 = t * 512
            def xtload(r0=r0):
                for b in range(4):
                    self.dma("sp", self.xt[:, b, :], xsrc[r0 + b * 128:r0 + (b + 1) * 128, :], "xt%d" % b, reads=[xsk], writes=["xt%d" % b])
            if ph == "A":
                xtload()
            self.dma("sp", self.cst[:, 0, :], self.cos_d[:, r0:r0 + 512], "cst", pwrites=["cst"])
            self.dma("sp", self.cst[:, 1, :], self.sin_d[:, r0:r0 + 512], "cst", pwrites=["cst"])
            nxt = t + 1 < NT
            if ph == "A":
                def st1(tb, s, r0=r0):
                    self.dma("act", self.x1[r0 + tb * 128:r0 + (tb + 1) * 128, s * 512:(s + 1) * 512],
                             self.xt[:, tb, s * 512:(s + 1) * 512], "x1_%d" % tb, reads=["xt%d" % tb], pwrites=["x1"])
                hooks = {}
                if nxt:
                    hooks[("gu", 1, 8)] = (lambda t=t: preA(t + 1, 0))
                    hooks[("gu", 1, 16)] = (lambda t=t: preA(t + 1, 1))
                    hooks[("dn", 1, 0)] = (lambda t=t: preB(0))
                    hooks[("dn", 1, 1)] = (lambda t=t: (preB(1), preA(t + 1, 2)))
                    hooks[("dn", 1, 2)] = (lambda t=t: preA(t + 1, 3))
                    hooks[("dn", 1, 3)] = (lambda t=t: preB(2))
                self.use_h(0)
                self.ffn("f1gu", "f1dn", store=st1, hooks=hooks)
                if nxt:
                    preB(3)
                self.use_h(1)
                self.norm_hT(1)
                self.proj_kvu(t)
            else:
                self.use_h(0)
                self.mixer(t, after_q=xtload)
                self.use_h(1)
                self.norm_hT(2)

                def st2(tb, s, r0=r0):
                    self.dma("act", self.y[r0 + tb * 128:r0 + (tb + 1) * 128, s * 512:(s + 1) * 512],
                             self.xt[:, tb, s * 512:(s + 1) * 512], "y_%d" % tb, reads=["xt%d" % tb], pwrites=["y"])
                hooks = {}
                if nxt:
                    for b, (pa, pb) in enumerate([(("gu", 0, 2), ("gu", 0, 7)), (("gu", 0, 11), ("gu", 0, 16)),
                                                  (("gu", 1, 2), ("gu", 1, 7)), (("gu", 1, 11), ("gu", 1, 16))]):
                        hooks[pa] = (lambda t=t, b=b: preA(t + 1, b))
                        hooks[pb] = (lambda b=b: preB(b))
                self.ffn("f2gu", "f2dn", store=st2, hooks=hooks)
        if ph == "B":
            self.outkeys.append("y")

    def proj_kvu(self, t):
        r0 = t * 512
        win = self.wscr["win"]
        descs = []
        for c in range(8):
            descs.append(([128, 16, 128], [(lambda v: v, win[:, 12 + c], self.wk("win", 12 + c))]))
        for g in range(2):
            descs.append(([128, 16, 128], [(lambda v: v, win[:, 8 + g], self.wk("win", 8 + g))]))
        descs.append(([128, 2, 16, 128], [(lambda v: v, win[:, 10:12], self.wk("win", 10))]))
        get = self.wstream(descs)
        for c in range(8):
            w, wk = get(c)
            b = self.bank()
            self.mm(b, [(w[:, k, :], self.hT[:, k, :]) for k in range(16)], reads=[wk, self.hk])
            self.actf(self.uT[:, c, :], self.ps[:, b, :], AF.Copy, reads=["ps%d" % b], writes=["uT%d" % c])
            self.dma("act", self.uS[c][:, r0:r0 + 512], self.uT[:, c, :], "uS%d" % c, reads=["uT%d" % c], pwrites=["uS"])
        for g in range(2):
            w, wk = get(8 + g)
            b = self.bank()
            self.mm(b, [(w[:, k, :], self.hT[:, k, :]) for k in range(16)], reads=[wk, self.hk])
            self.qk_head(b, 1, self.kT[:, g, :], "kT%d" % g, g % 2)
            self.dma("act", self.kS[g][:, r0:r0 + 512], self.kT[:, g, :], "kS%d" % g, reads=["kT%d" % g], pwrites=["kS"])
        w, wk = get(10)
        for tb in range(4):
            b = self.bank()
            self.mm(b, [(self.hT[:, k, tb * 128:(tb + 1) * 128], w[:, :, k, :]) for k in range(16)], reads=[wk, self.hk],
                    out=self.ps[:, b, 0:256])
            gb = t * 4 + tb
            self.ts("dve", self.vx[:, tb, :], self.ps[:, b, 0:256], self.valid[:, gb:gb + 1], ALU.mult,
                    reads=["ps%d" % b, "c_valid"], writes=["vx%d" % tb])
            self.dma("act", self.vS[r0 + tb * 128:r0 + (tb + 1) * 128, :], self.vx[:, tb, :], "vS%d" % tb, reads=["vx%d" % tb], pwrites=["vS"])

    def mixer(self, t, after_q=None):
        S, NB = self.S, self.NB
        r0 = t * 512
        win = self.wscr["win"]
        lo, hi = max(0, r0 - 128), min(S, r0 + 640)
        off = lo - (r0 - 128)
        nb_lo, nb_n = off // 128, (hi - lo) // 128
        for g in range(2):
            self.dma("sp", self.kwin[:, g, off:off + hi - lo], self.kS[g][:, lo:hi], "kwin", reads=["kS"], pwrites=["kwin"])
        self.dma("sp", self.vwin[:, nb_lo:nb_lo + nb_n, :], self.vS[lo:hi, :].rearrange("(b p) c -> p b c", p=128), "vwin",
                 reads=["vS"], pwrites=["vwin"])
        self.dma("sp", self.vrw[:, nb_lo:nb_lo + nb_n, :], self.vrep_d[lo:hi, :].rearrange("(b p) c -> p b c", p=128), "vrw",
                 pwrites=["vrw"])
        for o in range(8):
            self.dma("sp", self.z2T[:, o, :], self.ysS[o][:, r0:r0 + 512], "yld%d" % o, reads=["ysS"], pwrites=["z2T%d" % o])
        descs = []
        for h in range(8):
            descs.append(([128, 16, 128], [(lambda v: v, win[:, h], self.wk("win", h))]))
        for c in range(8):
            descs.append(([128, 8, 128], [(lambda v: v, self.wscr["wglu"][:, c], self.wk("wglu", c))]))
        for f in range(16):
            descs.append(([128, 48, 128], [
                (lambda v: v[:, 0:8], self.wscr["wao"][:, f], self.wk("wao", f)),
                (lambda v: v[:, 8:16], self.wscr["wso"][:, f], self.wk("wso", f)),
                (lambda v: v[:, 16:32], win[:, 20 + f], self.wk("win", 20 + f)),
                (lambda v: v[:, 32:48], win[:, 36 + f], self.wk("win", 36 + f))]))
        for s in range(4):
            for kh in range(2):
                descs.append(([128, 4, 8, 128], [(lambda v: v, self.wscr["wo"][:, 4 * s:4 * s + 4, kh * 8:kh * 8 + 8, :], self.wk("wo", 4 * s))]))
        get = self.wstream(descs)
        di = 0
        T = self.T
        zkeys = ["zT%d" % o for o in range(8)]

        def gelu_a(o):
            xv = self.z2T[:, o, :]
            ta, tb_ = self.sg[:, 0, :], self.sg[:, 1, :]
            self.tt("pool", ta, xv, xv, ALU.mult, reads=["z2T%d" % o], writes=["sg0"])
            self.ts("pool", tb_, ta, 0.044715, ALU.mult, reads=["sg0"], writes=["sg1"], s2=1.0, op1=ALU.add)
            self.tt("pool", self.zT[:, o, :], tb_, xv, ALU.mult, reads=["sg1", "z2T%d" % o], writes=["zT%d" % o])

        def gelu_b(o):
            xv = self.z2T[:, o, :]
            tb_, kb_ = self.sg[:, o % 2, :], "sg%d" % (o % 2)
            self.actf(tb_, self.zT[:, o, :], AF.Sigmoid, reads=["zT%d" % o], writes=[kb_], scale=1.5957691216057308)
            self.tt("pool", self.zT[:, o, :], xv, tb_, ALU.mult, reads=[kb_, "z2T%d" % o], writes=["zT%d" % o])

        def glu(c, di):
            w, wk = get(di)
            b = self.bank()
            self.mm(b, [(w[:, k, :], self.zT[:, k, :]) for k in range(8)], reads=[wk] + zkeys)
            sgl, ksg = self.sg[:, c % 2, :], "sg%d" % (c % 2)
            self.actf(sgl, self.ps[:, b, :], AF.Sigmoid, reads=["ps%d" % b, "c_small"], writes=[ksg], bias=self.small[:, 10 + c:11 + c])
            self.tt("pool", self.z2T[:, c, :], self.zT[:, c, :], sgl, ALU.mult, reads=[ksg, "zT%d" % c], writes=["z2T%d" % c])

        def qproj(h, di):
            w, wk = get(di)
            b = self.bank()
            self.mm(b, [(w[:, k, :], self.hT[:, k, :]) for k in range(16)], reads=[wk, self.hk])
            return b
        for o in range(8):
            gelu_a(o)
        qb_next = qproj(0, di)
        di += 1
        pend = None
        for h in range(8):
            qb_cur = qb_next
            if h + 1 < 8:
                qb_next = qproj(h + 1, di)
                di += 1
            sb_ = self.qk_head(qb_cur, 0, self.qT[:, h, :], "qT", h % 2, e2="dve", split=True)
            if pend is not None:
                pend()
            pend = sb_
        pend()
        if after_q is not None:
            after_q()
        sets = [(self.TTb[:, 0:3072].rearrange("p (g i n) -> p g i n", g=2, i=3), ["T0", "T1", "T2"]),
                (self.TTb[:, 3072:6144].rearrange("p (g i n) -> p g i n", g=2, i=3), ["T3", "T4", "T5"])]
        dn = self.TT[:, 3072:4096]
        dkeys = ["T6", "T7"]

        def evenpair():
            b0 = self.bank()
            if b0 % 2 == 1:
                b0 = self.bank()
            self.bank()
            return b0

        def s_stage(qb):
            gq = t * 4 + qb
            ds = [d for d in (-1, 0, 1) if 0 <= gq + d < NB]
            ex, ekeys = sets[qb % 2]
            for g in range(2):
                qv = self.qT[:, 4 * g:4 * g + 4, qb * 128:(qb + 1) * 128]
                for idx, d in enumerate(ds):
                    wb = qb + 1 + d
                    bS = self.bank()
                    self.mm(bS, [(self.kwin[:, g, wb * 128:(wb + 1) * 128], qv)], reads=["kwin", "qT"])
                    self.actf(ex[:, g, idx, :], self.ps[:, bS, :], AF.Exp, reads=["ps%d" % bS], pwrites=ekeys,
                              scale=float(128 ** -0.5))
            for idx, d in enumerate(ds):
                if d != 0:
                    ev = ex[:, :, idx, :].rearrange("p g (h q) -> p g h q", q=128)
                    mk = self.tri[:, 0 if d == -1 else 1, :].unsqueeze(1).unsqueeze(1).to_broadcast([128, 2, 4, 128])
                    self.tt("dve", ev, ev, mk, ALU.mult, reads=ekeys + ["c_tri"], pwrites=ekeys)

        def pv_stage(qb):
            gq = t * 4 + qb
            ds = [d for d in (-1, 0, 1) if 0 <= gq + d < NB]
            ex, ekeys = sets[qb % 2]
            bO = evenpair()
            bD = evenpair()
            for g in range(2):
                self.mm(bO + g, [(self.vwin[:, qb + 1 + d, g * 128:(g + 1) * 128], ex[:, g, idx, :]) for idx, d in enumerate(ds)],
                        reads=["vwin"] + ekeys)
                self.mm(bD + g, [(self.vrw[:, qb + 1 + d, :], ex[:, g, idx, :]) for idx, d in enumerate(ds)], reads=["vrw"] + ekeys)
            dv = dn.rearrange("p (g h q) -> p g h q", g=2, h=4)
            self.tt("dve", dv, self.ps[:, bD:bD + 2, :].rearrange("p g (h q) -> p g h q", q=128),
                    self.esink[:, 0:8].rearrange("p (g h) -> p g h", g=2).unsqueeze(3).to_broadcast([128, 2, 4, 128]), ALU.add,
                    reads=["ps%d" % bD, "ps%d" % (bD + 1), "esink"], writes=dkeys)
            self.op("dve", lambda e: e.reciprocal(out=dn, in_=dn), reads=dkeys, writes=dkeys)
            self.tt("dve", self.attnT[:, :, qb * 128:(qb + 1) * 128].rearrange("p (g h) q -> p g h q", g=2),
                    self.ps[:, bO:bO + 2, :].rearrange("p g (h q) -> p g h q", q=128), dv,
                    ALU.mult, reads=["ps%d" % bO, "ps%d" % (bO + 1)] + dkeys, pwrites=["attnT"])

        s_stage(0)
        for qb in range(4):
            if qb + 1 < 4:
                s_stage(qb + 1)
            pv_stage(qb)
        for o in range(8):
            gelu_b(o)
        for c in range(8):
            glu(c, di)
            di += 1
        z2keys = ["z2T%d" % o for o in range(8)]
        for f in range(16):
            w, wk = get(di)
            di += 1
            bA, bB, bG, bH = self.bank(), self.bank(), self.bank(), self.bank()
            self.mm(bG, [(w[:, 16 + k, :], self.hT[:, k, :]) for k in range(16)], reads=[wk, self.hk])
            self.mm(bH, [(w[:, 32 + k, :], self.hT[:, k, :]) for k in range(16)], reads=[wk, self.hk])
            self.mm(bA, [(w[:, k, :], self.attnT[:, k, :]) for k in range(8)], reads=[wk, "attnT"])
            self.mm(bB, [(w[:, 8 + k, :], self.z2T[:, k, :]) for k in range(8)], reads=[wk] + z2keys)
            s4 = (f % 2) * 4
            sa, ss_, m1, m2 = T[s4], T[s4 + 1], T[s4 + 2], T[s4 + 3]
            ka, ks_, k1, k2 = ["T%d" % (s4 + i) for i in range(4)]
            self.actf(sa, self.ps[:, bG, :], AF.Sigmoid, reads=["ps%d" % bG], writes=[ka])
            self.actf(ss_, self.ps[:, bH, :], AF.Sigmoid, reads=["ps%d" % bH], writes=[ks_])
            self.tt("dve", m1, sa, self.ps[:, bA, :], ALU.mult, reads=[ka, "ps%d" % bA], writes=[k1])
            self.tt("dve", m2, ss_, self.ps[:, bB, :], ALU.mult, reads=[ks_, "ps%d" % bB], writes=[k2])
            self.tt("pool", self.mergedT[:, f, :], m1, m2, ALU.add, reads=[k1, k2], writes=["act%d" % f])
        for s in range(4):
            banks = [self.bank() for _ in range(4)]
            for kh in range(2):
                w, wk = get(di)
                di += 1
                for tb in range(4):
                    pairs = [(self.mergedT[:, kh * 8 + j2, tb * 128:(tb + 1) * 128], w[:, :, j2, :]) for j2 in range(8)]
                    self.mm(banks[tb], pairs, reads=[wk] + ["act%d" % (kh * 8 + j2) for j2 in range(8)],
                            start=(kh == 0), stop=(kh == 1))
            for tb in range(4):
                sl = self.xt[:, tb, s * 512:(s + 1) * 512]
                self.tt("dve", sl, self.ps[:, banks[tb], :], sl, ALU.add, reads=["ps%d" % banks[tb], "xt%d" % tb],
                        pwrites=["xt%d" % tb])

    def frac(self, out, a, tmp, shift, rk, wk, tk):
        if shift != 0.0:
            self.ts("dve", out, a, shift, ALU.add, reads=[rk], writes=[wk])
            src, sk = out, wk
        else:
            src, sk = a, rk
        self.ts("dve", tmp, src, MAGIC, ALU.add, reads=[sk], writes=[tk])
        self.ts("dve", tmp, tmp, -MAGIC, ALU.add, reads=[tk], writes=[tk])
        self.tt("dve", out, src, tmp, ALU.subtract, reads=[sk, tk], writes=[wk])

    def cmul(self, o_re, o_im, a_re, a_im, b_re, b_im, t1, t2, reads, ok, tkeys, nim=None):
        k1, k2 = tkeys
        self.tt("dve", t1, a_re, b_re, ALU.mult, reads=reads, writes=[k1])
        self.tt("dve", t2, a_im, b_im, ALU.mult, reads=reads, writes=[k2])
        self.tt("dve", o_re, t1, t2, ALU.subtract, reads=[k1, k2], pwrites=[ok])
        self.tt("dve", t1, a_re, b_im, ALU.mult, reads=reads, writes=[k1])
        self.tt("dve", t2, a_im, b_re, ALU.mult, reads=reads, writes=[k2])
        self.tt("dve", o_im, t1, t2, ALU.add, reads=[k1, k2], pwrites=[ok])
        if nim is not None:
            self.ts("dve", nim, o_im, -1.0, ALU.mult, reads=[ok], pwrites=[ok])

    def cexp(self, R, n, lp, tag):
        A = lambda nm: R.alloc(tag + nm, [128, n], F32)
        lre, lim, ldt = lp[:, 0, :], lp[:, 1, :], lp[:, 2, :]
        ik = tag + "in"
        dt, t, mag, y, r1, r2, sn, cs = [A(x) for x in ("dt", "t", "mag", "y", "r1", "r2", "sn", "cs")]
        lbr, lbi, nr, den, fr, fi, u1, u2 = [A(x) for x in ("lbr", "lbi", "nr", "den", "fr", "fi", "u1", "u2")]
        K = lambda x: tag + x
        self.actf(dt, ldt, AF.Exp, reads=[ik], writes=[K("dt")])
        self.tt("dve", t, lre, dt, ALU.mult, reads=[ik, K("dt")], writes=[K("t")])
        self.actf(mag, t, AF.Exp, reads=[K("t")], writes=[K("mag")])
        self.stt(y, lim, 1.0 / TWO_PI, dt, ALU.mult, ALU.mult, reads=[ik, K("dt")], writes=[K("y")])
        self.frac(r1, y, r2, 0.0, K("y"), K("r1"), K("r2"))
        self.actf(sn, r1, AF.Sin, reads=[K("r1")], writes=[K("sn")], scale=TWO_PI)
        self.frac(r1, y, r2, 0.25, K("y"), K("r1"), K("r2"))
        self.actf(cs, r1, AF.Sin, reads=[K("r1")], writes=[K("cs")], scale=TWO_PI)
        self.tt("dve", lbr, mag, cs, ALU.mult, reads=[K("mag"), K("cs")], writes=[K("lbr")])
        self.tt("dve", lbi, mag, sn, ALU.mult, reads=[K("mag"), K("sn")], writes=[K("lbi")])
        self.ts("dve", nr, lbr, -1.0, ALU.add, reads=[K("lbr")], writes=[K("nr")])
        self.tt("dve", u1, lre, lre, ALU.mult, reads=[ik], writes=[K("u1")])
        self.tt("dve", u2, lim, lim, ALU.mult, reads=[ik], writes=[K("u2")])
        self.tt("dve", den, u1, u2, ALU.add, reads=[K("u1"), K("u2")], writes=[K("den")])
        self.op("dve", lambda e: e.reciprocal(out=den, in_=den), reads=[K("den")], writes=[K("den")])
        self.tt("dve", u1, nr, lre, ALU.mult, reads=[K("nr"), ik], writes=[K("u1")])
        self.tt("dve", u2, lbi, lim, ALU.mult, reads=[K("lbi"), ik], writes=[K("u2")])
        self.tt("dve", u1, u1, u2, ALU.add, reads=[K("u1"), K("u2")], writes=[K("u1")])
        self.tt("dve", fr, u1, den, ALU.mult, reads=[K("u1"), K("den")], writes=[K("fr")])
        self.tt("dve", u1, lbi, lre, ALU.mult, reads=[K("lbi"), ik], writes=[K("u1")])
        self.tt("dve", u2, nr, lim, ALU.mult, reads=[K("nr"), ik], writes=[K("u2")])
        self.tt("dve", u1, u1, u2, ALU.subtract, reads=[K("u1"), K("u2")], writes=[K("u1")])
        self.tt("dve", fi, u1, den, ALU.mult, reads=[K("u1"), K("den")], writes=[K("fi")])
        return dict(lbr=lbr, lbi=lbi, fr=fr, fi=fi, t=t, y=y, u1=u1, u2=u2, r1=r1, r2=r2,
                    keys=[K(x) for x in ("lbr", "lbi", "fr", "fi", "t", "y")])

    def phaseS(self):
        S = self.S
        N = S // 8
        NA = N // 16
        R = Region(self, self.cbase, "s")
        csl = R.alloc("c_sl", [128, 2, 64, 16], F32)
        maskp = R.alloc("maskp", [128, 4, 2], F32)
        pw = R.alloc("pw", [128, 9, 3, 64], F32)
        r8 = R.alloc("r8", [128, 64], F32)
        EA = R.alloc("EA", [128, 2, 64, 32], F32)
        EB = R.alloc("EB", [128, 2, 64, 16], F32)
        FB = R.alloc("FB", [128, 2, 64, 16], F32)
        base2 = R.off
        R2 = Region(self, base2, "spl")
        lp = R2.alloc("lp_sl", [128, 3, 64], F32)
        bsl = R2.alloc("b_sl", [128, 2, 64, 16], F32)
        iota = R2.alloc("iota", [128, 48], F32)
        self.dma("sp", lp, self.lp_sl_d, "s_lp", writes=["sl_in"])
        self.dma("sp", csl, self.c_sl_d, "s_c", writes=["csl"])
        self.dma("sp", bsl, self.b_sl_d, "s_b", writes=["bsl"])
        self.dma("sp", iota, self.iota_d, "s_io", writes=["iota"])
        self.dma("sp", maskp, self.maskp_d, "s_mk", writes=["maskp"])
        ce = self.cexp(R2, 64, lp, "sl_")
        self.op("dve", lambda e: e.memset(pw[:, 0, 0, :], 1.0), pwrites=["pw"])
        self.op("dve", lambda e: e.memset(pw[:, 0, 1:3, :], 0.0), pwrites=["pw"])
        for e_ in range(8):
            self.cmul(pw[:, e_ + 1, 0, :], pw[:, e_ + 1, 1, :], pw[:, e_, 0, :], pw[:, e_, 1, :], ce["lbr"], ce["lbi"],
                      ce["u1"], ce["u2"], ["pw"] + ce["keys"], "pw", ("sl_u1", "sl_u2"), nim=pw[:, e_ + 1, 2, :])
        self.actf(r8, ce["t"], AF.Exp, reads=ce["keys"], writes=["r8"], scale=8.0)
        ys8 = R2.alloc("ys8", [128, 64], F32)
        y8 = R2.alloc("y8", [128, 64], F32)
        self.ts("dve", y8, ce["y"], 8.0, ALU.mult, reads=ce["keys"], writes=["y8"])
        self.frac(ys8, y8, ce["r2"], 0.0, "y8", "ys8", "sl_r2")
        va = R2.alloc("va", [128, 64, 32], F32)
        vt = R2.alloc("vt", [128, 64, 32], F32)
        vr = R2.alloc("vr", [128, 64, 32], F32)
        for (tab, nn, i0, tk) in ((EA, 32, 0, "EA"), (EB, 16, 32, "EB")):
            v_, t_, r_ = va[:, :, 0:nn], vt[:, :, 0:nn], vr[:, :, 0:nn]
            self.tt("dve", v_, ys8.unsqueeze(2).to_broadcast([128, 64, nn]),
                    iota[:, i0:i0 + nn].unsqueeze(1).to_broadcast([128, 64, nn]), ALU.mult, reads=["ys8", "iota"], writes=["va"])
            self.frac(r_, v_, t_, 0.0, "va", "vr", "vt")
            self.actf(tab[:, 1], r_, AF.Sin, reads=["vr"], pwrites=[tk], scale=TWO_PI)
            self.frac(r_, v_, t_, 0.25, "va", "vr", "vt")
            self.actf(tab[:, 0], r_, AF.Sin, reads=["vr"], pwrites=[tk], scale=TWO_PI)
        fbt1 = R2.alloc("fbt1", [128, 64, 16], F32)
        fbt2 = R2.alloc("fbt2", [128, 64, 16], F32)
        self.cmul(FB[:, 0], FB[:, 1], ce["fr"].unsqueeze(2).to_broadcast([128, 64, 16]),
                  ce["fi"].unsqueeze(2).to_broadcast([128, 64, 16]), bsl[:, 0], bsl[:, 1], fbt1, fbt2,
                  ce["keys"] + ["bsl"], "FB", ("fbt1", "fbt2"))
        lpp = R2.alloc("lp_pl", [128, 3, 1024], F32)
        bpl = R2.alloc("b_pl", [128, 2, 1024], F32)
        self.dma("sp", lpp, self.lp_pl_d, "s_lpp", writes=["pl_in"])
        self.dma("sp", bpl, self.b_pl_d, "s_bpl", writes=["bpl"])
        cp = self.cexp(R2, 1024, lpp, "pl_")
        W = [R2.alloc("W%d" % i, [128, 2, 1024], F32) for i in range(2)]
        pwo = [R2.alloc("pwo%d" % i, [128, 2, 1024], BF16) for i in range(2)]
        self.op("dve", lambda e: e.tensor_copy(out=W[0][:, 0, :], in_=cp["fr"]), reads=cp["keys"], pwrites=["W0"])
        self.op("dve", lambda e: e.tensor_copy(out=W[0][:, 1, :], in_=cp["fi"]), reads=cp["keys"], pwrites=["W0"])
        for e_ in range(8):
            wc, wn = W[e_ % 2], W[(e_ + 1) % 2]
            kc, kn = "W%d" % (e_ % 2), "W%d" % ((e_ + 1) % 2)
            po, pk = pwo[e_ % 2], "pwo%d" % (e_ % 2)
            self.cmul(po[:, 0, :], po[:, 1, :], wc[:, 0, :], wc[:, 1, :], bpl[:, 0, :], bpl[:, 1, :], cp["u1"], cp["u2"],
                      [kc, "bpl"], pk, ("pl_u1", "pl_u2"))
            self.dma("sp", self.pwS[:, e_], po, pk, reads=[pk], pwrites=["pwS"])
            if e_ < 7:
                self.cmul(wn[:, 0, :], wn[:, 1, :], wc[:, 0, :], wc[:, 1, :], cp["lbr"], cp["lbi"], cp["u1"], cp["u2"],
                          [kc] + cp["keys"], kn, ("pl_u1", "pl_u2"))
        self.sc.barrier()
        R3 = Region(self, base2, "so")
        U = R3.alloc("U", [128, S], BF16)
        U8 = R3.alloc("U8", [128, 8, N], BF16)
        PWo = R3.alloc("PWo", [128, 8, 2, 128], BF16)
        PWt = [R3.alloc("PWt%d" % i, [128, 8, 2, 2, 128], BF16) for i in range(2)]
        QWt = R3.alloc("QWt", [128, 9 * 2 * 2 * 4 * 128], BF16)
        BWt = R3.alloc("BWt", [128, 2 * 2 * 4 * 128], BF16)
        KT = R3.alloc("KT", [128, 16, 128], BF16)
        Yo = U
        E2 = [R3.alloc("E%d" % i, [128, 2, 2, N], F32) for i in range(2)]
        T1 = R3.alloc("T1", [128, 2, N], F32)
        T2 = R3.alloc("T2", [128, 2, N], F32)
        G1 = R3.alloc("G1", [128, 2, N], F32)
        G2 = R3.alloc("G2", [128, 2, N], F32)
        P1 = R3.alloc("P1", [128, 2, N], F32)
        P2 = R3.alloc("P2", [128, 2, N], F32)
        Wb = R3.alloc("Wb", [128, 2, 2, N], F32)
        Vb = R3.alloc("Vb", [128, 2, 2, N], F32)
        Xb = R3.alloc("Xb", [128, 8, 2, N + 2], BF16)
        ct = [R3.alloc("ct%d" % i, [128, 3, 8, 16], F32) for i in range(4)]
        self.op("pool", lambda e: e.memset(QWt, 0.0), writes=["QWt"])
        self.op("pool", lambda e: e.memset(BWt, 0.0), writes=["BWt"])
        self.op("pool", lambda e: e.memset(Xb, 0.0), writes=["Xb"])
        QW5 = QWt.rearrange("p (e d r q c) -> p e d r q c", e=9, d=2, r=2, q=4)
        BW4 = BWt.rearrange("p (d r q c) -> p d r q c", d=2, r=2, q=4)
        unit = 0
        for o in range(8):
            self.dma("sp", U, self.uS[o], "s_U", reads=["uS"], writes=["U"])
            self.dma("sp", PWo, self.pwS[:, :, :, o * 128:(o + 1) * 128], "s_PWo", reads=["pwS"], writes=["PWo"])
            self.op("act", lambda e: e.activation(out=U8, in_=U.rearrange("p (n i) -> p i n", i=8), func=AF.Copy),
                    reads=["U"], writes=["U8"])
            pd0 = o * 8
            Cre, Cim = csl[:, 0, pd0:pd0 + 8, :], csl[:, 1, pd0:pd0 + 8, :]

            def qfill(es, pd0=pd0, Cre=Cre, Cim=Cim):
                ne, e0 = len(es), es[0]
                Cr = Cre.unsqueeze(1).to_broadcast([128, ne, 8, 16])
                Ci = Cim.unsqueeze(1).to_broadcast([128, ne, 8, 16])
                Pre = pw[:, e0:e0 + ne, 0, pd0:pd0 + 8].unsqueeze(3).to_broadcast([128, ne, 8, 16])
                Pim = pw[:, e0:e0 + ne, 1, pd0:pd0 + 8].unsqueeze(3).to_broadcast([128, ne, 8, 16])
                Pni = pw[:, e0:e0 + ne, 2, pd0:pd0 + 8].unsqueeze(3).to_broadcast([128, ne, 8, 16])
                c0, c1, c2, c3 = [ct[k][:, 0:ne] for k in range(4)]
                self.tt("pool", c0, Cr, Pre, ALU.mult, reads=["csl", "pw"], writes=["ct0"])
                self.tt("pool", c1, Ci, Pim, ALU.mult, reads=["csl", "pw"], writes=["ct1"])
                self.tt("pool", c2, Cr, Pni, ALU.mult, reads=["csl", "pw"], writes=["ct2"])
                self.tt("pool", c3, Ci, Pre, ALU.mult, reads=["csl", "pw"], writes=["ct3"])
                for g2 in range(2):
                    for reim in range(2):
                        dst = _cap(QWt, 64 * g2, 64, e0 * 2048 + reim * 512 + 16 * g2, [[1024, 2 * ne], [160, 4], [1, 16]])
                        a_, b_ = (ct[0], ct[1]) if reim == 0 else (ct[2], ct[3])
                        rowc = a_.ap[0][0]
                        av = bass.AP(tensor=a_.tensor, offset=a_.offset + 64 * g2 * rowc, ap=[[rowc, 64], [64, 2 * ne], [16, 4], [1, 16]])
                        bv = bass.AP(tensor=b_.tensor, offset=b_.offset + 64 * g2 * rowc, ap=[[rowc, 64], [64, 2 * ne], [16, 4], [1, 16]])
                        self.tt("pool", dst, av, bv, ALU.subtract, reads=["ct0", "ct1", "ct2", "ct3"], pwrites=["QWt"])

            def ktstage(pd0=pd0):
                for g2 in range(2):
                    for reim in range(2):
                        dst = _cap(BWt, 64 * g2, 64, reim * 512 + 16 * g2, [[1024, 2], [160, 4], [1, 16]])
                        srcv = FB[64 * g2:64 * g2 + 64, reim, pd0:pd0 + 8, :].rearrange("p (d q) c -> p d q c", d=2)
                        self.op("pool", lambda e, dst=dst, srcv=srcv: e.tensor_copy(out=dst, in_=srcv),
                                reads=["FB"], pwrites=["BWt"])
                kbanks = [self.bank() for _ in range(4)]
                for lag in range(15):
                    dl = lag - 7
                    dirs = [0] if dl > 0 else ([1] if dl < 0 else [0, 1])
                    pairs = []
                    for d_ in dirs:
                        for q in range(4):
                            for reim in range(2):
                                pairs.append((BW4[:, d_, reim, q, :], QW5[:, abs(dl), d_, reim, q, :]))
                    self.mm(kbanks[lag // 4], pairs, reads=["BWt", "QWt"],
                            out=self.ps[:, kbanks[lag // 4], (lag % 4) * 128:(lag % 4 + 1) * 128])
                    if lag % 4 == 3 or lag == 14:
                        kb = kbanks[lag // 4]
                        nl = lag % 4 + 1
                        self.actf(KT[:, (lag // 4) * 4:(lag // 4) * 4 + nl, :],
                                  self.ps[:, kb, 0:nl * 128].rearrange("p (l c) -> p l c", c=128), AF.Copy,
                                  reads=["ps%d" % kb], pwrites=["KT"])
            qparts = [[0, 1, 2], [3, 4], [5, 6], [7, 8]]
            ui = 0
            for d_ in range(2):
                for h in range(2):
                    pt, ptk = PWt[unit % 2], "PWt%d" % (unit % 2)
                    b0 = (unit % 2) * 4
                    unit += 1
                    for reim in range(2):
                        rowP = PWo.ap[0][0]
                        e0 = 7 if d_ == 0 else 0
                        est = -256 if d_ == 0 else 256
                        src_ = bass.AP(tensor=PWo.tensor, offset=PWo.offset + e0 * 256 + reim * 128 + d_ * 64,
                                       ap=[[rowP, 128], [est, 8], [0, 4], [1, 64]])
                        rowT = pt.ap[0][0]
                        dst_ = bass.AP(tensor=pt.tensor, offset=pt.offset + reim * 256, ap=[[rowT, 128], [512, 8], [64, 4], [1, 64]])
                        rowM = maskp.ap[0][0]
                        msk_ = bass.AP(tensor=maskp.tensor, offset=maskp.offset + 4 * h, ap=[[rowM, 128], [0, 8], [1, 4], [0, 64]])
                        self.tt("pool", dst_, src_, msk_, ALU.mult, reads=["PWo", "maskp"], pwrites=[ptk])
                    for reim in range(2):
                        for ql in range(2):
                            bk = b0 + reim * 2 + ql
                            self.mm(bk, [(pt[:, i, reim, ql, :], U8[:, i, :]) for i in range(8)], reads=[ptk, "U8"],
                                    out=self.ps[:, bk, 0:N])
                    qfill(qparts[ui])
                    ui += 1
                    pdl = d_ * 4 + 2 * h
                    pdg = o * 8 + pdl
                    E, ek = E2[unit % 2], "E%d" % (unit % 2)
                    Ev = E.rearrange("p c l (a b) -> p c l a b", b=16)
                    def eab(tab, c):
                        return tab[:, c, pdg:pdg + 2, 0:NA].unsqueeze(3).to_broadcast([128, 2, NA, 16])
                    def ebb(c):
                        return EB[:, c, pdg:pdg + 2, :].unsqueeze(2).to_broadcast([128, 2, NA, 16])
                    t1v = G1.rearrange("p l (a b) -> p l a b", b=16)
                    t2v = G2.rearrange("p l (a b) -> p l a b", b=16)
                    self.tt("dve", t1v, eab(EA, 0), ebb(0), ALU.mult, reads=["EA", "EB"], writes=["G1"])
                    self.tt("dve", t2v, eab(EA, 1), ebb(1), ALU.mult, reads=["EA", "EB"], writes=["G2"])
                    self.tt("dve", Ev[:, 0], t1v, t2v, ALU.subtract, reads=["G1", "G2"], pwrites=[ek])
                    p1v = P1.rearrange("p l (a b) -> p l a b", b=16)
                    p2v = P2.rearrange("p l (a b) -> p l a b", b=16)
                    self.tt("pool", p1v, eab(EA, 1), ebb(0), ALU.mult, reads=["EA", "EB"], writes=["P1"])
                    self.tt("pool", p2v, eab(EA, 0), ebb(1), ALU.mult, reads=["EA", "EB"], writes=["P2"])
                    self.tt("pool", Ev[:, 1], p1v, p2v, ALU.add, reads=["P1", "P2"], pwrites=[ek])
                    Ec = E[:, 0] if d_ == 0 else E[:, 0, :, ::-1]
                    Es = E[:, 1] if d_ == 0 else E[:, 1, :, ::-1]
                    zr = self.ps[:, b0:b0 + 2, 0:N]
                    zi = self.ps[:, b0 + 2:b0 + 4, 0:N]
                    kzr = ["ps%d" % b0, "ps%d" % (b0 + 1)]
                    kzi = ["ps%d" % (b0 + 2), "ps%d" % (b0 + 3)]
                    self.tt("dve", T1, zr, Ec, ALU.mult, reads=kzr + [ek], writes=["T1"])
                    self.tt("dve", T2, zi, Es, ALU.mult, reads=kzi + [ek], writes=["T2"])
                    self.tt("dve", Wb[:, :, 0, :], T1, T2, ALU.add, reads=["T1", "T2"], pwrites=["Wb"])
                    self.tt("dve", T1, zi, Ec, ALU.mult, reads=kzi + [ek], writes=["T1"])
                    self.tt("dve", T2, zr, Es, ALU.mult, reads=kzr + [ek], writes=["T2"])
                    self.tt("dve", Wb[:, :, 1, :], T1, T2, ALU.subtract, reads=["T1", "T2"], pwrites=["Wb"])
                    for ql in range(2):
                        for reim in range(2):
                            wv, vv = Wb[:, ql, reim, :], Vb[:, ql, reim, :]
                            if d_ == 1:
                                wv, vv = wv[:, ::-1], vv[:, ::-1]
                            dec = r8[:, pdg + ql:pdg + ql + 1].to_broadcast([128, N])
                            self.op("dve", lambda e, wv=wv, vv=vv, dec=dec: e.tensor_tensor_scan(
                                out=vv, data0=dec, data1=wv, initial=0.0, op0=ALU.mult, op1=ALU.add),
                                reads=["Wb", "r8"], pwrites=["Vb"])
                    xr = Xb[:, pdl:pdl + 2, 0, 1:N + 1]
                    xi = Xb[:, pdl:pdl + 2, 1, 1:N + 1]
                    self.tt("dve", T1, Vb[:, :, 0, :], Ec, ALU.mult, reads=["Vb", ek], writes=["T1"])
                    self.tt("dve", T2, Vb[:, :, 1, :], Es, ALU.mult, reads=["Vb", ek], writes=["T2"])
                    self.tt("dve", xr, T1, T2, ALU.subtract, reads=["T1", "T2"], pwrites=["Xb"])
                    self.tt("dve", G1, Vb[:, :, 0, :], Es, ALU.mult, reads=["Vb", ek], writes=["G1"])
                    self.tt("dve", G2, Vb[:, :, 1, :], Ec, ALU.mult, reads=["Vb", ek], writes=["G2"])
                    self.tt("dve", xi, G1, G2, ALU.add, reads=["G1", "G2"], pwrites=["Xb"])
            ktstage()
            Y8 = Yo.rearrange("p (n i) -> p i n", i=8)
            for j in range(8):
                pairs = [(KT[:, (j - i) + 7, :], U8[:, i, :]) for i in range(8)]
                for d_ in range(2):
                    e_ = (j + 1) if d_ == 0 else (8 - j)
                    off = 0 if d_ == 0 else 2
                    for q in range(4):
                        for reim in range(2):
                            pairs.append((QW5[:, e_, d_, reim, q, :], Xb[:, d_ * 4 + q, reim, off:off + N]))
                bk = self.bank()
                self.mm(bk, pairs, reads=["KT", "U8", "QWt", "Xb"], out=self.ps[:, bk, 0:N])
                self.stt(Y8[:, j, :], U8[:, j, :], self.small[:, 18 + o:19 + o], self.ps[:, bk, 0:N], ALU.mult, ALU.add,
                         reads=["U8", "c_small", "ps%d" % bk], pwrites=["U"])
            self.dma("act", self.ysS[o], Yo, "s_Y", reads=["U"], pwrites=["ysS"])
            if self.dbg_barrier:
                self.sc.barrier()


def _bf16(a):
    import ml_dtypes
    return np.asarray(a).astype(ml_dtypes.bfloat16)


def host_consts(S):
    c = {}
    c["ident"] = _bf16(np.eye(128, dtype=np.float32))
    c["onesm"] = _bf16(np.ones((128, 128), np.float32))
    rot = np.zeros((128, 128), np.float32)
    for m in range(64):
        rot[m + 64, m] = -1.0
        rot[m, m + 64] = 1.0
    c["rotm"] = rot
    pos = np.arange(S, dtype=np.float32)
    inv = (10000.0 ** (-np.arange(0, 128, 2, dtype=np.float32) / 128)).astype(np.float32)
    ang = pos[None, :] * inv[:, None]
    c["cosT"] = np.concatenate([np.cos(ang), np.cos(ang)], 0).astype(np.float32)
    c["sinT"] = np.concatenate([np.sin(ang), np.sin(ang)], 0).astype(np.float32)
    kl = np.arange(128)[:, None]
    ql = np.arange(128)[None, :]
    c["trimask"] = _bf16(np.stack([(kl >= ql), (kl <= ql)], 1).astype(np.float32))
    c["iotaab"] = np.tile(np.concatenate([16.0 * np.arange(32), np.arange(16)]).astype(np.float32)[None, :], (128, 1))
    mp = np.zeros((128, 4, 2), np.float32)
    for r in range(128):
        gl = r // 16
        mp[r, gl // 2, gl % 2] = 1.0
    c["maskp"] = mp
    return c


def host_params(p):
    o = {}
    o["w_f1gu"] = p["ffn1_w_gate_up"]; o["w_f1dn"] = p["ffn1_w_down"]; o["w_win"] = p["w_in"]
    o["w_wglu"] = p["w_glu"]; o["w_wao"] = p["w_attn_out"]; o["w_wso"] = p["w_ssm_out"]; o["w_wo"] = p["w_out"]
    o["w_f2gu"] = p["ffn2_w_gate_up"]; o["w_f2dn"] = p["ffn2_w_down"]
    g3 = np.stack([p["ffn1_norm"], p["mix_norm"], p["ffn2_norm"]], 0)
    o["gT3"] = np.ascontiguousarray(g3.reshape(3, 16, 128).transpose(2, 0, 1))
    sm = np.zeros((128, 32), np.float32)
    sm[:, 0] = p["q_norm"]; sm[:, 1] = p["k_norm"]
    sm[:, 2:10] = p["attn_sink"][None, :]
    sm[:, 10:18] = p["b_glu"].reshape(8, 128).T
    sm[:, 18:26] = p["ssm_d"].reshape(8, 128).T
    o["small"] = sm
    lre, lim, ldt = p["ssm_lambda_re"], p["ssm_lambda_im"], p["ssm_log_dt"]
    bre, bim, cre, cim = p["ssm_b_re"], p["ssm_b_im"], p["ssm_c_re"], p["ssm_c_im"]
    def pl(a3):
        t = a3.reshape(2, 8, 8, 64)
        t = t.transpose(2, 1, 0, 3)
        t = np.repeat(t[:, None], 16, axis=1)
        return t.reshape(128, 1024)
    ldt3 = np.repeat(ldt[:, :, None], 64, axis=2)
    o["lp_pl"] = np.ascontiguousarray(np.stack([pl(lre), pl(lim), pl(ldt3)], 1)).astype(np.float32)
    def plb(b4):
        t = b4.reshape(2, 8, 8, 64, 16).transpose(2, 4, 1, 0, 3)
        return t.reshape(128, 1024)
    o["b_pl"] = np.ascontiguousarray(np.stack([plb(bre), plb(bim)], 1)).astype(np.float32)
    def sl(a3):
        t = a3.reshape(2, 8, 4, 2, 64)
        t = t.transpose(3, 4, 1, 0, 2)
        return t.reshape(128, 64)
    o["lp_sl"] = np.ascontiguousarray(np.stack([sl(lre), sl(lim), sl(ldt3)], 1)).astype(np.float32)
    def slc(c4):
        t = c4.reshape(2, 8, 4, 2, 16, 64).transpose(3, 5, 1, 0, 2, 4)
        return t.reshape(128, 64, 16)
    def slb(b4):
        t = b4.reshape(2, 8, 4, 2, 64, 16).transpose(3, 4, 1, 0, 2, 5)
        return t.reshape(128, 64, 16)
    o["c_sl"] = np.ascontiguousarray(np.stack([slc(cre), slc(cim)], 1)).astype(np.float32)
    o["b_sl"] = np.ascontiguousarray(np.stack([slb(bre), slb(bim)], 1)).astype(np.float32)
    return o


def core_inputs(xc, nvalid, S, consts, params):
    m = dict(consts)
    m.update(params)
    m["x"] = np.ascontiguousarray(xc, dtype=np.float32)
    v = (np.arange(S) < nvalid).astype(np.float32)
    m["validcol"] = np.ascontiguousarray(v.reshape(S // 128, 128).T)
    m["validrep"] = _bf16(np.repeat(v[:, None], 128, axis=1))
    return m


def kernel(**inputs):
    S = 4096
    x_prompt = np.asarray(inputs["x_prompt"], dtype=np.float32)
    x_sample = np.asarray(inputs["x_sample"], dtype=np.float32)
    p = {k: np.asarray(v)[0] for k, v in inputs.items() if k not in ("x_prompt", "x_sample")}
    consts = host_consts(S)
    params = host_params(p)
    in_maps = []
    for i in range(4):
        in_maps.append(core_inputs(x_prompt[i], S, S, consts, params))
    ns = x_sample.shape[1]
    for i in range(4):
        xp = np.zeros((S, D), np.float32)
        xp[:ns] = x_sample[i]
        in_maps.append(core_inputs(xp, ns, S, consts, params))
    prog = Prog(S)
    res = run_bass_kernel_spmd(prog.nc, in_maps, core_ids=list(range(8)))
    y_prompt = np.stack([np.asarray(res.results[i]["y"], dtype=np.float32) for i in range(4)], 0)
    y_sample = np.stack([np.asarray(res.results[4 + i]["y"], dtype=np.float32)[:ns] for i in range(4)], 0)
    return (y_prompt, y_sample)
```
